# Optimizing a Trainium2 kernel written in Bass

```python
import math
import jax, jax.numpy as jnp
from jax import lax
import numpy as np

D_MODEL = 2048
BATCH = 16
SEQ = 2048
DEPTH = 2

EXPAND = 2
D_INNER = EXPAND * D_MODEL
N_MIXERS = 2
N_A = (DEPTH + 1) // 2
N_B = DEPTH // 2
N_HEADS = 32
HEAD_DIM = D_INNER // N_HEADS
KV_LORA = 256
IDX_HEADS = 16
IDX_DIM = 64
TOP_K_MAX = 256
Q_BLOCK = 128
A_SIZES = (D_INNER, KV_LORA, IDX_HEADS * IDX_DIM, IDX_DIM, IDX_HEADS, D_INNER)
A_IN = D_INNER + KV_LORA + IDX_HEADS * IDX_DIM + IDX_DIM + IDX_HEADS + D_INNER
A_SPLITS = (D_INNER,
            D_INNER + KV_LORA,
            D_INNER + KV_LORA + IDX_HEADS * IDX_DIM,
            D_INNER + KV_LORA + IDX_HEADS * IDX_DIM + IDX_DIM,
            D_INNER + KV_LORA + IDX_HEADS * IDX_DIM + IDX_DIM + IDX_HEADS)
REL_BUCKETS = 32
REL_MAX_DIST = 128
POOL_WINDOWS = (2, 4, 8, 16)
N_POOL_GROUPS = 4
POOL_GROUP = D_INNER // N_POOL_GROUPS
EPS = 1e-6

kernel_name = "hybrid_dsa_pool_interleaved"


def rmsnorm(x, g):
    xf = x.astype(jnp.float32)
    y = xf * lax.rsqrt(jnp.mean(xf * xf, axis=-1, keepdims=True) + EPS)
    return (y * g.astype(jnp.float32)).astype(x.dtype)


def t5_bucket(dist):
    max_exact = REL_BUCKETS // 2
    d = jnp.maximum(dist, 0)
    df = jnp.maximum(d, 1).astype(jnp.float32)
    large = max_exact + (jnp.log(df / max_exact) / math.log(REL_MAX_DIST / max_exact)
                         * (REL_BUCKETS - max_exact)).astype(jnp.int32)
    large = jnp.minimum(large, REL_BUCKETS - 1)
    return jnp.where(d < max_exact, d, large)


def dsa_mixer(h, w_in, kv_norm, kidx_norm, w_uk, w_uv, w_out, rel_bias):
    B, L, _ = h.shape
    proj = h @ w_in
    q, c_kv, q_idx, k_idx, w_idx, z = jnp.split(proj, list(A_SPLITS), axis=-1)
    q = q.reshape(B, L, N_HEADS, HEAD_DIM)
    c_kv = rmsnorm(c_kv, kv_norm)
    q_idx = q_idx.reshape(B, L, IDX_HEADS, IDX_DIM)
    k_idx = rmsnorm(k_idx, kidx_norm).astype(jnp.float32)
    w_idx = w_idx * (IDX_HEADS ** -0.5)
    top_k = min(TOP_K_MAX, L // 4)
    n_blk = L // Q_BLOCK
    key_pos = jnp.arange(L)

    def to_blocks(a):
        return a.reshape(B, n_blk, Q_BLOCK, *a.shape[2:]).swapaxes(0, 1)

    def block_fn(args):
        blk, qb, qib, wb = args
        t_pos = blk * Q_BLOCK + jnp.arange(Q_BLOCK)
        s = jnp.einsum("bthd,bsd->bths", qib.astype(jnp.float32), k_idx) * (IDX_DIM ** -0.5)
        score = jnp.einsum("bth,bths->bts", wb.astype(jnp.float32), jax.nn.relu(s))
        causal = key_pos[None, :] <= t_pos[:, None]
        score = jnp.where(causal[None], score, -jnp.inf)
        _, idx = lax.top_k(score, top_k)
        valid = idx <= t_pos[None, :, None]
        c_sel = jax.vmap(lambda c, i: c[i])(c_kv, idx)
        q_lat = jnp.einsum("bthd,chd->bthc", qb, w_uk)
        logits = jnp.einsum("bthc,btkc->bthk", q_lat, c_sel).astype(jnp.float32) * (HEAD_DIM ** -0.5)
        bias = rel_bias[t5_bucket(t_pos[None, :, None] - idx)]
        logits = logits + jnp.swapaxes(bias, 2, 3).astype(jnp.float32)
        logits = jnp.where(valid[:, :, None, :], logits, -jnp.inf)
        p = jax.nn.softmax(logits, axis=-1).astype(c_sel.dtype)
        o_lat = jnp.einsum("bthk,btkc->bthc", p, c_sel)
        return jnp.einsum("bthc,chd->bthd", o_lat, w_uv)

    out = lax.map(block_fn, (jnp.arange(n_blk), to_blocks(q), to_blocks(q_idx), to_blocks(w_idx)))
    out = out.swapaxes(0, 1).reshape(B, L, D_INNER)
    y = out * jax.nn.silu(z)
    return y @ w_out


def pool_mixer(h, w_in, w_grp, b_grp, scale, w_out):
    B, L, _ = h.shape
    u, z = jnp.split(h @ w_in, 2, axis=-1)
    ug = u.reshape(B, L, N_POOL_GROUPS, POOL_GROUP).astype(jnp.float32)
    cs = jnp.concatenate([jnp.zeros_like(ug[:, :1]), jnp.cumsum(ug, axis=1)], axis=1)
    pos = jnp.arange(L)
    win = jnp.array(POOL_WINDOWS, dtype=jnp.int32)
    start = jnp.maximum(pos[:, None] - win[None, :] + 1, 0)
    cnt = (pos[:, None] - start + 1).astype(jnp.float32)
    lo = cs[:, start, jnp.arange(N_POOL_GROUPS)]
    pooled = (cs[:, 1:] - lo) / cnt[None, :, :, None] - ug
    mixed = jnp.einsum("blgp,gpq->blgq", pooled, w_grp.astype(jnp.float32)) + b_grp.astype(jnp.float32)
    mixed = mixed.reshape(B, L, D_INNER) * scale.astype(jnp.float32)
    y = mixed.astype(h.dtype) * jax.nn.silu(z)
    return y @ w_out


def setup_inputs(seed: int = 0) -> dict:
    key = jax.random.key(seed)
    ks = jax.random.split(key, 20)
    f32 = jnp.float32
    nrm = lambda k, shp, s: jax.random.normal(k, shp, f32) * s
    return {
        "x": nrm(ks[0], (BATCH, SEQ, D_MODEL), 1.0),
        "norm_a": 1.0 + nrm(ks[1], (N_A, D_MODEL), 0.05),
        "w_in_a": nrm(ks[2], (N_A, D_MODEL, A_IN), D_MODEL ** -0.5),
        "kv_norm_a": 1.0 + nrm(ks[3], (N_A, KV_LORA), 0.05),
        "kidx_norm_a": 1.0 + nrm(ks[4], (N_A, IDX_DIM), 0.05),
        "w_uk_a": nrm(ks[5], (N_A, KV_LORA, N_HEADS, HEAD_DIM), KV_LORA ** -0.5),
        "w_uv_a": nrm(ks[6], (N_A, KV_LORA, N_HEADS, HEAD_DIM), KV_LORA ** -0.5),
        "w_out_a": nrm(ks[7], (N_A, D_INNER, D_MODEL), D_INNER ** -0.5),
        "norm_b": 1.0 + nrm(ks[8], (N_B, D_MODEL), 0.05),
        "w_in_b": nrm(ks[9], (N_B, D_MODEL, 2 * D_INNER), D_MODEL ** -0.5),
        "w_grp_b": nrm(ks[10], (N_B, N_POOL_GROUPS, POOL_GROUP, POOL_GROUP), POOL_GROUP ** -0.5),
        "b_grp_b": nrm(ks[11], (N_B, N_POOL_GROUPS, POOL_GROUP), 0.02),
        "scale_b": 1.0 + nrm(ks[12], (N_B, D_INNER), 0.1),
        "w_out_b": nrm(ks[13], (N_B, D_INNER, D_MODEL), D_INNER ** -0.5),
        "rel_bias": nrm(ks[14], (REL_BUCKETS, N_HEADS), 0.5),
        "final_norm": 1.0 + nrm(ks[15], (D_MODEL,), 0.05),
    }


def reference(x, norm_a, w_in_a, kv_norm_a, kidx_norm_a, w_uk_a, w_uv_a, w_out_a,
              norm_b, w_in_b, w_grp_b, b_grp_b, scale_b, w_out_b, rel_bias, final_norm):
    h = x
    for i in range(DEPTH):
        j = i // N_MIXERS
        if i % N_MIXERS == 0:
            h = h + dsa_mixer(rmsnorm(h, norm_a[j]), w_in_a[j], kv_norm_a[j], kidx_norm_a[j],
                              w_uk_a[j], w_uv_a[j], w_out_a[j], rel_bias)
        else:
            h = h + pool_mixer(rmsnorm(h, norm_b[j]), w_in_b[j], w_grp_b[j], b_grp_b[j],
                               scale_b[j], w_out_b[j])
    return rmsnorm(h, final_norm)
```

```python
import numpy as np
from contextlib import ExitStack
import concourse.bass as bass
import concourse.mybir as mybir
from concourse.bass_utils import run_bass_kernel_spmd

F32 = mybir.dt.float32
BF16 = mybir.dt.bfloat16
AF = mybir.ActivationFunctionType
ALU = mybir.AluOpType
AX = mybir.AxisListType

SAME_ENGINE_SYNC = True
NIT = 16
TOPK = 256
L = 2048
D = 2048
T = 512
NTILE = L // T
EPS = 1e-6
NEG = -1.0e30


class Buf:
    __slots__ = ("name", "w", "r", "psum")

    def __init__(self, name, psum=False):
        self.name = name
        self.w = None
        self.r = {}
        self.psum = psum


class KB:
    def __init__(self, nc, es):
        self.nc = nc
        self.es = es
        self.eng = dict(pe=nc.tensor, act=nc.scalar, dve=nc.vector, pool=nc.gpsimd, sp=nc.sync)
        self.sem = {k: es.enter_context(nc.semaphore("s_" + k)) for k in self.eng}
        self.cnt = {k: 0 for k in self.eng}
        self.seen = {k: {} for k in self.eng}
        self.dsems = {}
        self.nwaits = 0
        self.nins = 0

    def sb(self, name, shape, dt, es=None):
        return (es or self.es).enter_context(self.nc.sbuf_tensor("sb_" + name, list(shape), dt))

    def ps(self, name, shape, dt):
        return self.es.enter_context(self.nc.psum_tensor("ps_" + name, list(shape), dt))

    def need(self, e, tok):
        key, sem, val = tok
        if key == e and (e == "pe" or not SAME_ENGINE_SYNC):
            return
        if self.seen[e].get(key, 0) >= val:
            return
        self.eng[e].wait_ge(sem, val)
        self.seen[e][key] = val
        self.nwaits += 1

    def _deps(self, e, reads, writes):
        for b in reads:
            if b.w is not None:
                self.need(e, b.w)
            if b.psum:
                for kk, t in b.r.items():
                    if kk != e:
                        self.need(e, t)
        for b in writes:
            if b.w is not None:
                self.need(e, b.w)
            for t in b.r.values():
                self.need(e, t)

    def op(self, e, reads, writes, fn, nosame=False):
        if nosame:
            sv = self.seen[e].get(e, 0)
            self.seen[e][e] = 1 << 60
        self._deps(e, reads, writes)
        if nosame:
            self.seen[e][e] = sv
        ins = fn(self.eng[e])
        self.cnt[e] += 1
        ins.then_inc(self.sem[e], 1)
        tok = (e, self.sem[e], self.cnt[e])
        for b in reads:
            b.r[e] = tok
        for b in writes:
            b.w = tok
            b.r = {}
        self.nins += 1
        return ins

    def dma(self, q, out_ap, in_ap, reads, writes, dsem, **kw):
        key0 = "d:" + dsem
        for b in reads:
            if b.w is not None:
                self.need(q, b.w)
        for b in writes:
            if b.w is not None and b.w[0] != key0:
                self.need(q, b.w)
            for t in b.r.values():
                self.need(q, t)
        if dsem not in self.dsems:
            self.dsems[dsem] = [self.es.enter_context(self.nc.semaphore("d_" + dsem)), 0]
        ds = self.dsems[dsem]
        ins = self.eng[q].dma_start(out=out_ap, in_=in_ap, **kw)
        ds[1] += 16
        ins.then_inc(ds[0], 16)
        key = "d:" + dsem
        tok = (key, ds[0], ds[1])
        for b in reads:
            b.r[key] = tok
        for b in writes:
            b.w = tok
            b.r = {}
        self.nins += 1
        return ins

    def finish(self, e, bufs):
        for b in bufs:
            if b.w is not None:
                self.need(e, b.w)
            for t in b.r.values():
                self.need(e, t)

    def barrier(self, bufs):
        for e in ("pe", "act", "dve", "pool", "sp"):
            self.finish(e, bufs)


WSPEC_L0 = dict(wq=(32, 2048), wz=(32, 2048), wqi=(8, 2048), wsm=(1, 16 * 336), wuk=(32, 256), wuv=(32, 256),
                wo=(16, 4096), bias=(32, 256))
WSPEC_L1 = dict(wu=(32, 2048), wz1=(32, 2048), wg=(32, 1024), wo1=(16, 4096))
GFOLD = dict(wq="ga", wz="ga", wqi="ga", wsm="ga", wu="gb", wz1="gb")
GFOLD_N = dict(wq=128, wz=128, wqi=128, wsm=336, wu=128, wz1=128)


def build(NB, do_l0=True, do_l1=True):
    nc = bass.Bass("TRN2", target_bir_lowering=False)
    dram = lambda n, s, kind="ExternalInput", dt=F32: nc.dram_tensor(n, list(s), dt, kind=kind).ap()
    x_d = dram("x", [NB, L, D])
    out_d = dram("out", [NB, L, D], kind="ExternalOutput")
    wspec = {}
    if do_l0:
        wspec.update(WSPEC_L0)
    if do_l1:
        wspec.update(WSPEC_L1)
    wsrc = {n: dram(n, [nb, 128, F]) for n, (nb, F) in wspec.items()}
    wdst = {n: dram(n + "_b", [nb, 128, F], kind="Internal", dt=BF16) for n, (nb, F) in wspec.items()}
    ident_d = dram("ident", [128, 128])
    gaT_d = dram("gaT", [128, 16])
    gbT_d = dram("gbT", [128, 16])
    gkv_d = dram("gkv", [128, 256])
    gki_d = dram("gki", [128, 64])
    gfin_d = dram("gfin", [128, 2048])
    rb31_d = dram("rb31", [128, 32])
    bgrp_d = dram("bgrp", [128, 32])
    scb_d = dram("scb", [128, 32])
    cneg_d = dram("cneg", [128, 128])
    icnt_d = dram("icnt", [128, 64])
    cfs_d = dram("cfs", [128, 32])

    with ExitStack() as es:
        k = KB(nc, es)

        def const(name, src, shape):
            t = k.sb(name, shape, F32)
            b = Buf(name)
            k.dma("sp", t[:], src, [], [b], "c_" + name)
            return t, b

        idf, b_idf = const("idf", ident_d[:, :], [128, 128])
        ga, b_ga = const("ga", gaT_d[:, :], [128, 16])
        gb, b_gb = const("gb", gbT_d[:, :], [128, 16])
        gkv, b_gkv = const("gkv", gkv_d[:, :], [128, 256])
        gki, b_gki = const("gki", gki_d[:, :], [128, 64])
        rb31, b_rb31 = const("rb31", rb31_d[:, :], [128, 32])
        bgrp, b_bgrp = const("bgrp", bgrp_d[:, :], [128, 32])
        scb, b_scb = const("scb", scb_d[:, :], [128, 32])
        cneg, b_cneg = const("cneg", cneg_d[:, :], [128, 128])
        icnt, b_icnt = const("icnt", icnt_d[:, :], [128, 64])
        cfs, b_cfs = const("cfs", cfs_d[:, :], [128, 32])
        idb = k.sb("idb", [128, 128], BF16); b_idb = Buf("idb")
        onesb = k.sb("onesb", [128, 128], BF16); b_onesb = Buf("onesb")
        onesf = k.sb("onesf", [128, 128], F32); b_onesf = Buf("onesf")
        k.op("dve", [b_idf], [b_idb], lambda e: e.tensor_copy(out=idb[:], in_=idf[:]))
        k.op("dve", [], [b_onesb], lambda e: e.memset(onesb[:], 1.0))
        k.op("dve", [], [b_onesf], lambda e: e.memset(onesf[:], 1.0))

        b_wd = {n: Buf("wd_" + n) for n in wspec}
        cast_order = [n for n in ("wsm", "wqi", "wuk", "wuv", "wq", "wz", "wo", "wu", "wg", "wz1", "wo1") if n in wspec]
        late_casts = [n for n in cast_order if n in WSPEC_L1] if do_l0 else []
        cast_state = {"late_done": not late_casts}

        def emit_casts(names, dep):
            for b in dep:
                if b.w is not None:
                    k.need("pool", b.w)
            for n in names:
                cast_one(n, [])

        def cast_one(n, dep):
            nb, F = wspec[n]
            rows = nb * 128
            nsplit = max(1, (rows * F * 4) // (8 << 20))
            while rows % nsplit:
                nsplit += 1
            rstep = rows // nsplit
            srcv = wsrc[n].rearrange("b p f -> (b p) f")
            dstv = wdst[n].rearrange("b p f -> (b p) f")
            for r0 in range(0, rows, rstep):
                k.dma("pool", dstv[r0:r0 + rstep, :], srcv[r0:r0 + rstep, :], dep, [b_wd[n]], "cw_" + n, max_dma_last_dim=4096)

        for n in cast_order:
            if n in late_casts:
                continue
            cast_one(n, [])
        for n in []:
            nb, F = wspec[n]
            rows = nb * 128
            nsplit = max(1, (rows * F * 4) // (8 << 20))
            while rows % nsplit:
                nsplit += 1
            rstep = rows // nsplit
            srcv = wsrc[n].rearrange("b p f -> (b p) f")
            dstv = wdst[n].rearrange("b p f -> (b p) f")
            for r0 in range(0, rows, rstep):
                k.dma("pool", dstv[r0:r0 + rstep, :], srcv[r0:r0 + rstep, :], [], [b_wd[n]], "cw_" + n, max_dma_last_dim=4096)

        xs = k.sb("xs", [128, 4, D], F32); b_xs = [Buf(f"xs{i}") for i in range(4)]
        xT = k.sb("xT", [128, 16, T], BF16); b_xT = Buf("xT")
        yT = k.sb("yT", [128, 16, T], BF16); b_yT = [Buf(f"yT{i}") for i in range(16)]
        rstd = k.sb("rstd", [128, 4], F32); b_rstd = Buf("rstd")
        rbc = k.sb("rbc", [128, T], F32); b_rbc = Buf("rbc")
        sml = k.sb("sml", [128, 64], F32); b_sml = Buf("sml")
        NWB = 3
        wbuf = [k.sb(f"wbuf{i}", [128, 16, 128], BF16) for i in range(NWB)]; b_wbuf = [Buf(f"wbuf{i}") for i in range(NWB)]
        wob = [k.sb(f"wob{i}", [128, 8, 512], BF16) for i in range(2)]; b_wob = [Buf(f"wob{i}") for i in range(2)]
        wrr = [0, 0]
        pb = [k.ps(f"pb{i}", [128, 512], F32) for i in range(8)]; b_pb = [Buf(f"pb{i}", psum=True) for i in range(8)]
        prr = [0]
        ptr = [0, 0]

        def bank():
            i = 3 + prr[0] % 5
            prr[0] += 1
            return pb[i], b_pb[i]

        def load_w(name, blk):
            i = wrr[0] % NWB
            wrr[0] += 1
            k.dma("sp", wbuf[i][:].rearrange("p c n -> p (c n)"), wdst[name][blk, :, :], [b_wd[name]], [b_wbuf[i]], f"wb{i}")
            return wbuf[i], b_wbuf[i]

        SCR = 67 * 1024
        scr = k.sb("scr", [128, SCR // 2], BF16)
        b_scr_all = []

        class Carver:
            def __init__(self):
                self.off = 0
                self.bufs = []

            def get(self, name, cols, dt):
                nbytes = cols * (4 if dt == F32 else 2)
                nbytes = (nbytes + 31) // 32 * 32
                assert self.off + nbytes <= SCR, (name, self.off, nbytes)
                v = scr[:, self.off // 2:(self.off + nbytes) // 2]
                self.off += nbytes
                if dt == F32:
                    v = v.bitcast(F32)
                v = v[:, 0:cols]
                b = Buf(name)
                self.bufs.append(b)
                return v, b

        if do_l0:
            ckv_tok = k.sb("ckv_tok", [128, 16, 256], BF16); b_ckv_tok = Buf("ckv_tok")
            ckvT = k.sb("ckvT", [128, 2, L], BF16); b_ckvT = Buf("ckvT")
            kidxT = k.sb("kidxT", [128, L], BF16); b_kidxT = Buf("kidxT")
            wsm = k.sb("wsm", [128, 16, 336], BF16); b_wsm = Buf("wsm")
            k.dma("sp", wsm[:].rearrange("p c n -> p (c n)"), wdst["wsm"][0, :, :], [b_wd["wsm"]], [b_wsm], "wsm")
            wkv = [k.sb(f"wkv{i}", [128, 512], BF16) for i in range(2)]; b_wkv = [Buf(f"wkv{i}") for i in range(2)]
            bsb = [k.sb(f"bsb{i}", [128, 256], BF16) for i in range(2)]; b_bsb = [Buf(f"bsb{i}") for i in range(2)]
            bsf = [k.sb(f"bsf{i}", [128, 256], F32) for i in range(2)]; b_bsf = [Buf(f"bsf{i}") for i in range(2)]
            c0 = Carver()
            maskT, b_maskT = c0.get("maskT", 16 * T, BF16)
            maskT = maskT.rearrange("p (j t) -> p j t", j=16)
            accraw, b_acc = c0.get("acc", L, F32)
            acc1, b_acc1 = c0.get("acc1", L, F32)
            acc = accraw
            xb0 = accraw.bitcast(BF16)[:, 0:D]; b_xb0 = b_acc
            msk, b_msk = c0.get("msk", L, BF16)
            junk0, b_junk0 = msk, b_msk
            qiT_flat, b_qiT = c0.get("qiT", 8 * T, BF16)
            qiT = qiT_flat.rearrange("p (j t) -> p j t", j=8)
            ktok, b_ktok = c0.get("ktok", 128, BF16)
            wts, b_wts = c0.get("wts", 64, F32)
            bis, b_bis = c0.get("bis", 16 + 64, F32)
            qTh = []; gth = []; qlT = []; pTs = []
            for i in range(2):
                qTh.append(c0.get(f"qTh{i}", T, BF16))
                gth.append(c0.get(f"gth{i}", T, BF16))
                v, b = c0.get(f"qlT{i}", 2 * T, BF16)
                qlT.append((v.rearrange("p (c t) -> p c t", c=2), b))
            zs0, b_zs0 = c0.get("zs0", T, F32)
            ez0, b_ez0 = c0.get("ez0", T, F32)
            for i in range(4):
                pTs.append(c0.get(f"pT{i}", T, BF16))
            rden, b_rden = c0.get("rden", T, F32)
            olat, b_olat = c0.get("olat", 2 * T, BF16)
            olat = olat.rearrange("p (c t) -> p c t", c=2)
            dg0, b_dg0 = c0.get("dg0", 128, F32)
            b_scr_all += c0.bufs
            print("scratch L0 bytes", c0.off)
        if do_l1:
            c1 = Carver()
            pooled2 = []
            for i in range(2):
                v, b = c1.get(f"pooled{i}", 8 * T, BF16)
                pooled2.append((v.rearrange("p (c t) -> p c t", c=8), b))
            ue = [c1.get(f"ue{i}", 16 + T, F32) for i in range(2)]
            sB, b_sB = c1.get("sB", 16 + T, F32)
            sC, b_sC = c1.get("sC", 16 + T, F32)
            zs1, b_zs1 = c1.get("zs1", T, F32)
            ez1, b_ez1 = c1.get("ez1", T, F32)
            gt1, b_gt1 = c1.get("gt1", T, F32)
            mx1, b_mx1 = c1.get("mx1", T, F32)
            xb1, b_xb1 = c1.get("xb1", D, BF16)
            junk1, b_junk1 = c1.get("junk1", D, BF16)
            dg1, b_dg1 = c1.get("dg1", 128, F32)
            gfin, b_gfin = c1.get("gfin", D, F32)
            ost = [c1.get(f"ost{i}", D, F32) for i in range(2)]
            halo = k.sb("halo", [128, 32, 16], F32); b_halo = Buf("halo")
            wgb = [k.sb(f"wgb{i}", [128, 8, 128], BF16) for i in range(2)]; b_wgb = [Buf(f"wgb{i}") for i in range(2)]
            b_scr_all += c1.bufs
            print("scratch L1 bytes", c1.off)

        def norm_and_transpose(xbs, junk_v, b_junk, dg_v, b_dg, gt, b_gt):
            for tb in range(4):
                xb_v, b_xb = xbs[tb % len(xbs)]
                k.op("act", [b_xs[tb]], [b_junk, b_sml],
                     lambda e: e.activation(out=junk_v[:, 0:D], in_=xs[:, tb, :], func=AF.Square, accum_out=sml[:, tb:tb + 1]))
                k.op("act", [b_sml], [b_sml],
                     lambda e: e.activation(out=sml[:, 4 + tb:5 + tb], in_=sml[:, tb:tb + 1], func=AF.Sqrt, scale=1.0 / D, bias=EPS))
                k.op("dve", [b_sml], [b_rstd], lambda e: e.reciprocal(out=rstd[:, tb:tb + 1], in_=sml[:, 4 + tb:5 + tb]))
                k.op("dve", [b_xs[tb]], [b_xb], lambda e: e.tensor_copy(out=xb_v[:, 0:D], in_=xs[:, tb, :]))
                for half in range(2):
                    bk, b_bk = bank()
                    bkb = bk.bitcast(BF16)
                    for j in range(8):
                        c = half * 8 + j
                        k.op("pe", [b_xb, b_idb], [b_bk],
                             lambda e: e.transpose(out=bkb[:, j * 128:(j + 1) * 128], in_=xb_v[:, c * 128:(c + 1) * 128], identity=idb[:]))
                    for j in range(8):
                        c = half * 8 + j
                        dst = xT[:, c, tb * 128:(tb + 1) * 128]
                        srcj = bkb[:, j * 128:(j + 1) * 128]
                        if j % 2 == 0:
                            k.op("act", [b_bk, b_gt], [b_xT], lambda e: e.activation(out=dst, in_=srcj, func=AF.Copy, scale=gt[:, c:c + 1]), nosame=(j > 0))
                        else:
                            k.op("dve", [b_bk, b_gt], [b_xT],
                                 lambda e: e.tensor_scalar(out=dst, in0=srcj, scalar1=gt[:, c:c + 1], scalar2=None, op0=ALU.mult), nosame=(j > 1))
            bk_r, b_bk_r = bank()
            for tb in range(4):
                k.op("dve", [b_idf, b_rstd], [b_dg],
                     lambda e: e.tensor_scalar(out=dg_v[:, 0:128], in0=idf[:], scalar1=rstd[:, tb:tb + 1], scalar2=None, op0=ALU.mult))
                k.op("pe", [b_dg, b_onesf], [b_bk_r],
                     lambda e: e.matmul(bk_r[:, tb * 128:(tb + 1) * 128], lhsT=onesf[:], rhs=dg_v[:, 0:128], start=True, stop=True))
            k.op("act", [b_bk_r], [b_rbc], lambda e: e.activation(out=rbc[:], in_=bk_r[:], func=AF.Copy))

        def proj_fm(wt, b_wt, bk, b_bk):
            for c in range(16):
                k.op("pe", [b_wt, b_xT], [b_bk], lambda e: e.matmul(bk[:], lhsT=wt[:, c, :], rhs=xT[:, c, :], start=(c == 0), stop=(c == 15)))

        def silu_gate(bk, b_bk, zs_v, b_zs, ez_v, b_ez, out_v, b_out):
            k.op("dve", [b_bk, b_rbc], [b_zs],
                 lambda e: e.scalar_tensor_tensor(out=zs_v[:, 0:T], in0=bk[:], scalar=0.5, in1=rbc[:], op0=ALU.mult, op1=ALU.mult))
            k.op("act", [b_zs], [b_ez], lambda e: e.activation(out=ez_v[:, 0:T], in_=zs_v[:, 0:T], func=AF.Tanh))
            k.op("dve", [b_ez, b_zs], [b_out],
                 lambda e: e.scalar_tensor_tensor(out=out_v[:, 0:T], in0=ez_v[:, 0:T], scalar=1.0, in1=zs_v[:, 0:T], op0=ALU.add, op1=ALU.mult))

        def w_out_half(wname, hf):
            for dmc in range(4):
                accb = [bank() for _ in range(4)]
                for gr in range(2):
                    i = wrr[1] % 2
                    wrr[1] += 1
                    blk = (hf * 4 + dmc) * 2 + gr
                    k.dma("sp", wob[i][:].rearrange("p c n -> p (c n)"), wdst[wname][blk, :, :], [b_wd[wname]], [b_wob[i]], f"wo{i}")
                    for tb in range(4):
                        for ic in range(8):
                            ci = gr * 8 + ic
                            k.op("pe", [b_wob[i], b_yT[ci]], [accb[tb][1]],
                                 lambda e: e.matmul(accb[tb][0][:], lhsT=yT[:, ci, tb * 128:(tb + 1) * 128], rhs=wob[i][:, ic, :],
                                                    start=(ci == 0), stop=(ci == 15)))
                for tb in range(4):
                    k.op("dve", [accb[tb][1], b_xs[tb]], [b_xs[tb]],
                         lambda e: e.tensor_tensor(out=xs[:, tb, dmc * 512:(dmc + 1) * 512], in0=accb[tb][0][:],
                                                   in1=xs[:, tb, dmc * 512:(dmc + 1) * 512], op=ALU.add))

        def layer0(ti):
            norm_and_transpose([(xb0, b_xb0), (acc1.bitcast(BF16)[:, 0:D], b_acc1)], junk0, b_junk0, dg0, b_dg0, ga, b_ga)
            for tb in range(4):
                gbk = ti * 4 + tb
                bk, b_bk = bank()
                for c in range(16):
                    k.op("pe", [b_xT, b_wsm], [b_bk],
                         lambda e: e.matmul(bk[:, 0:336], lhsT=xT[:, c, tb * 128:(tb + 1) * 128], rhs=wsm[:, c, :], start=(c == 0), stop=(c == 15)))
                k.op("act", [b_bk], [b_junk0, b_sml],
                     lambda e: e.activation(out=junk0[:, 0:256], in_=bk[:, 0:256], func=AF.Square, accum_out=sml[:, 8:9]))
                k.op("act", [b_bk], [b_junk0, b_sml],
                     lambda e: e.activation(out=junk0[:, 256:320], in_=bk[:, 256:320], func=AF.Square, accum_out=sml[:, 9:10]))
                k.op("dve", [b_rstd], [b_sml], lambda e: e.tensor_tensor(out=sml[:, 10:11], in0=rstd[:, tb:tb + 1], in1=rstd[:, tb:tb + 1], op=ALU.mult))
                k.op("dve", [b_sml], [b_sml], lambda e: e.tensor_scalar(out=sml[:, 11:13], in0=sml[:, 8:10], scalar1=sml[:, 10:11], scalar2=None, op0=ALU.mult))
                k.op("act", [b_sml], [b_sml], lambda e: e.activation(out=sml[:, 13:14], in_=sml[:, 11:12], func=AF.Sqrt, scale=1.0 / 256, bias=EPS))
                k.op("act", [b_sml], [b_sml], lambda e: e.activation(out=sml[:, 14:15], in_=sml[:, 12:13], func=AF.Sqrt, scale=1.0 / 64, bias=EPS))
                k.op("dve", [b_sml], [b_sml], lambda e: e.reciprocal(out=sml[:, 15:17], in_=sml[:, 13:15]))
                k.op("dve", [b_sml, b_rstd], [b_sml], lambda e: e.tensor_scalar(out=sml[:, 17:19], in0=sml[:, 15:17], scalar1=rstd[:, tb:tb + 1], scalar2=None, op0=ALU.mult))
                k.op("dve", [b_bk, b_sml, b_gkv], [b_ckv_tok],
                     lambda e: e.scalar_tensor_tensor(out=ckv_tok[:, gbk, :], in0=bk[:, 0:256], scalar=sml[:, 17:18], in1=gkv[:], op0=ALU.mult, op1=ALU.mult))
                for rep in range(2):
                    k.op("dve", [b_bk, b_sml, b_gki], [b_ktok],
                         lambda e: e.scalar_tensor_tensor(out=ktok[:, rep * 64:(rep + 1) * 64], in0=bk[:, 256:320], scalar=sml[:, 18:19], in1=gki[:], op0=ALU.mult, op1=ALU.mult))
                k.op("act", [b_bk], [b_wts], lambda e: e.activation(out=wts[:, tb * 16:(tb + 1) * 16], in_=bk[:, 320:336], func=AF.Copy))
                bk2, b_bk2 = bank()
                bk2b = bk2.bitcast(BF16)
                for cc in range(2):
                    k.op("pe", [b_ckv_tok, b_idb], [b_bk2],
                         lambda e: e.transpose(out=bk2b[:, cc * 128:(cc + 1) * 128], in_=ckv_tok[:, gbk, cc * 128:(cc + 1) * 128], identity=idb[:]))
                k.op("pe", [b_ktok, b_idb], [b_bk2], lambda e: e.transpose(out=bk2b[:, 256:384], in_=ktok[:, 0:128], identity=idb[:]))
                k.op("act", [b_bk2], [b_ckvT],
                     lambda e: e.activation(out=ckvT[:, :, gbk * 128:(gbk + 1) * 128], in_=bk2b[:, 0:256].rearrange("p (c t) -> p c t", c=2), func=AF.Copy))
                k.op("dve", [b_bk2], [b_kidxT], lambda e: e.tensor_copy(out=kidxT[:, gbk * 128:(gbk + 1) * 128], in_=bk2b[:, 256:384]))
            for j in range(8):
                wt, b_wt = load_w("wqi", j)
                bk, b_bk = bank()
                proj_fm(wt, b_wt, bk, b_bk)
                if j % 2 == 0:
                    k.op("act", [b_bk], [b_qiT], lambda e: e.activation(out=qiT[:, j, :], in_=bk[:], func=AF.Copy))
                else:
                    k.op("dve", [b_bk], [b_qiT], lambda e: e.tensor_copy(out=qiT[:, j, :], in_=bk[:]))
            accs = [(acc, b_acc), (acc1, b_acc1)]
            tmps = [(zs0, b_zs0), (ez0, b_ez0), (rden, b_rden)]

            def indexer(qb, filler=None):
                gbk = ti * 4 + qb
                S = (gbk + 1) * 128
                a_v, b_a = accs[qb % 2]
                bo_ = 8 * (qb % 2)
                nparts_tot = 16 * ((S + 511) // 512)
                stride = max(1, nparts_tot // (NIT + 2))
                pcount = [0]
                for h in range(16):
                    pr = 64 * (h % 2)
                    for s0 in range(0, S, 512):
                        n = min(512, S - s0)
                        bk, b_bk = bank()
                        k.op("pe", [b_qiT, b_kidxT], [b_bk],
                             lambda e: e.matmul(bk[:, 0:n], lhsT=qiT[pr:pr + 64, h // 2, qb * 128:(qb + 1) * 128], rhs=kidxT[pr:pr + 64, s0:s0 + n],
                                                start=True, stop=True))
                        wcol = wts[:, qb * 16 + h:qb * 16 + h + 1]
                        if h == 0:
                            k.op("dve", [b_bk, b_wts], [b_a],
                                 lambda e: e.tensor_scalar(out=a_v[:, s0:s0 + n], in0=bk[:, 0:n], scalar1=0.0, scalar2=wcol, op0=ALU.max, op1=ALU.mult))
                        else:
                            k.op("act", [b_bk], [b_bk], lambda e: e.activation(out=bk[:, 0:n], in_=bk[:, 0:n], func=AF.Relu))
                            k.op("dve", [b_bk, b_wts, b_a], [b_a],
                                 lambda e: e.scalar_tensor_tensor(out=a_v[:, s0:s0 + n], in0=bk[:, 0:n], scalar=wcol, in1=a_v[:, s0:s0 + n],
                                                                  op0=ALU.mult, op1=ALU.add))
                        pcount[0] += 1
                        if filler is not None and pcount[0] % stride == 0:
                            next(filler, None)
                if filler is not None:
                    for _ in filler:
                        pass
                k.op("dve", [b_a], [b_bis], lambda e: e.tensor_reduce(out=bis[:, bo_:bo_ + 1], in_=a_v[:, 0:S], axis=AX.X, op=ALU.max, apply_absolute_value=True))
                k.op("dve", [b_bis], [b_bis], lambda e: e.tensor_scalar(out=bis[:, bo_ + 1:bo_ + 2], in0=bis[:, bo_:bo_ + 1], scalar1=2.0, scalar2=1.0, op0=ALU.mult, op1=ALU.add))
                nd0 = 16 + 32 * (qb % 2)
                k.op("dve", [b_bis, b_cfs], [b_bis],
                     lambda e: e.tensor_scalar(out=bis[:, nd0:nd0 + 32], in0=cfs[:, 0:32], scalar1=bis[:, bo_ + 1:bo_ + 2], scalar2=-1.0, op0=ALU.mult, op1=ALU.mult))
                k.op("dve", [b_a, b_cneg], [b_a],
                     lambda e: e.tensor_tensor(out=a_v[:, gbk * 128:(gbk + 1) * 128], in0=a_v[:, gbk * 128:(gbk + 1) * 128], in1=cneg[:], op=ALU.add))

            def bisect_gen(qb):
                gbk = ti * 4 + qb
                S = (gbk + 1) * 128
                a_v, b_a = accs[qb % 2]
                bo_ = 8 * (qb % 2)
                nd0 = 16 + 32 * (qb % 2)
                nm = [bis[:, bo_ + 2:bo_ + 3], bis[:, bo_ + 3:bo_ + 4]]
                sg = bis[:, bo_ + 4:bo_ + 5]
                s1 = bis[:, bo_ + 5:bo_ + 6]
                thr = bis[:, bo_ + 6:bo_ + 7]
                if S > TOPK:
                    k.op("act", [], [b_bis], lambda e: e.activation(out=nm[0], in_=cfs[:, 0:1], func=AF.Copy))
                    for it in range(NIT):
                        cur = nm[it % 2]
                        nxt = nm[(it + 1) % 2]
                        k.op("act", [b_a, b_bis], [b_msk, b_bis],
                             lambda e: e.activation(out=msk[:, 0:S], in_=a_v[:, 0:S], func=AF.Sign, bias=cur, accum_out=sg))
                        k.op("act", [b_bis], [b_bis], lambda e: e.activation(out=s1, in_=sg, func=AF.Sign, bias=float(S - 2 * TOPK + 1)))
                        k.op("act", [b_bis], [b_bis],
                             lambda e: e.activation(out=nxt, in_=s1, func=AF.Identity, scale=bis[:, nd0 + it + 1:nd0 + it + 2], bias=cur))
                        yield
                    fin = nm[NIT % 2]
                    k.op("dve", [b_bis], [b_bis],
                         lambda e: e.tensor_scalar(out=thr, in0=fin, scalar1=-1.0, scalar2=bis[:, nd0 + NIT:nd0 + NIT + 1], op0=ALU.mult, op1=ALU.add))
                else:
                    k.op("dve", [b_bis], [b_bis], lambda e: e.tensor_scalar(out=thr, in0=bis[:, bo_:bo_ + 1], scalar1=-1.0, scalar2=-1.0, op0=ALU.mult, op1=ALU.add))

            def mask_tr(qb):
                gbk = ti * 4 + qb
                S = (gbk + 1) * 128
                a_v, b_a = accs[qb % 2]
                bo_ = 8 * (qb % 2)
                thr = bis[:, bo_ + 6:bo_ + 7]
                k.op("dve", [b_a, b_bis], [b_msk],
                     lambda e: e.tensor_scalar(out=msk[:, 0:S], in0=a_v[:, 0:S], scalar1=thr, scalar2=None, op0=ALU.is_gt))
                for j0 in range(0, gbk + 1, 8):
                    nj = min(8, gbk + 1 - j0)
                    bk, b_bk = bank()
                    bkb = bk.bitcast(BF16)
                    for jj in range(nj):
                        j = j0 + jj
                        k.op("pe", [b_msk, b_idb], [b_bk],
                             lambda e: e.transpose(out=bkb[:, jj * 128:(jj + 1) * 128], in_=msk[:, j * 128:(j + 1) * 128], identity=idb[:]))
                    k.op("act", [b_bk], [b_maskT],
                         lambda e: e.activation(out=maskT[:, j0:j0 + nj, qb * 128:(qb + 1) * 128],
                                                in_=bkb[:, 0:nj * 128].rearrange("p (j t) -> p j t", j=nj), func=AF.Copy))

            nchunk = ti * 4 + 4
            sc = 128.0 ** -0.5
            LA = 3

            def stageA(h):
                par = h % 2
                wq_t, b_wq = load_w("wq", h)
                wz_t, b_wz = load_w("wz", h)
                k.dma("sp", wkv[par][:, 0:256], wdst["wuk"][h, :, :], [b_wd["wuk"]], [b_wkv[par]], f"wkv{par}")
                k.dma("sp", wkv[par][:, 256:512], wdst["wuv"][h, :, :], [b_wd["wuv"]], [b_wkv[par]], f"wkv{par}")
                k.dma("sp", bsf[par][:, :], wsrc["bias"][h, :, :], [], [b_bsf[par]], f"bsf{par}")
                k.op("dve", [b_bsf[par], b_rb31], [b_bsb[par]],
                     lambda e: e.tensor_scalar(out=bsb[par][:, :], in0=bsf[par][:, :], scalar1=rb31[:, h:h + 1], scalar2=float(128.0 ** 0.5),
                                               op0=ALU.subtract, op1=ALU.mult))
                qT_v, b_qT = qTh[par]
                g_v, b_g = gth[par]
                ql_v, b_ql = qlT[par]
                bq, b_bq = bank()
                proj_fm(wq_t, b_wq, bq, b_bq)
                k.op("dve", [b_bq, b_rbc], [b_qT], lambda e: e.tensor_tensor(out=qT_v[:, 0:T], in0=bq[:], in1=rbc[:], op=ALU.mult))
                bz, b_bz = bank()
                proj_fm(wz_t, b_wz, bz, b_bz)
                silu_gate(bz, b_bz, zs0, b_zs0, ez0, b_ez0, g_v, b_g)
                for cc in range(2):
                    bl, b_bl = bank()
                    k.op("pe", [b_wkv[par], b_qT], [b_bl],
                         lambda e: e.matmul(bl[:], lhsT=wkv[par][:, cc * 128:(cc + 1) * 128], rhs=qT_v[:, 0:T], start=True, stop=True))
                    k.op("act", [b_bl], [b_ql], lambda e: e.activation(out=ql_v[:, cc, :], in_=bl[:], func=AF.Copy))

            def logits(h, j):
                par = h % 2
                ql_v, b_ql = qlT[par]
                jl = j - ti * 4
                c0_ = max(0, jl) * 128
                lg, b_lg = bank()
                near = []
                for qbl in range(4):
                    dist = (ti * 4 + qbl) - j
                    if dist in (0, 1):
                        near.append((qbl, dist))
                for cc in range(2):
                    k.op("pe", [b_ckvT, b_ql], [b_lg],
                         lambda e: e.matmul(lg[:, c0_:T], lhsT=ckvT[:, cc, j * 128:(j + 1) * 128], rhs=ql_v[:, cc, c0_:T],
                                            start=(cc == 0), stop=(cc == 1 and not near)))
                for ni, (qbl, dist) in enumerate(near):
                    k.op("pe", [b_idb, b_bsb[par]], [b_lg],
                         lambda e: e.matmul(lg[:, qbl * 128:(qbl + 1) * 128], lhsT=idb[:], rhs=bsb[par][:, dist * 128:(dist + 1) * 128],
                                            start=False, stop=(ni == len(near) - 1)))
                pT_v, b_pT = pTs[ptr[0] % 4]
                ptr[0] += 1
                k.op("act", [b_lg, b_rb31], [b_pT],
                     lambda e: e.activation(out=pT_v[:, c0_:T], in_=lg[:, c0_:T], func=AF.Exp, scale=sc, bias=rb31[:, h:h + 1]))
                k.op("dve", [b_pT, b_maskT], [b_pT],
                     lambda e: e.tensor_tensor(out=pT_v[:, c0_:T], in0=pT_v[:, c0_:T], in1=maskT[:, j, c0_:T], op=ALU.mult))
                return (j, c0_, pT_v, b_pT)

            def pv(item):
                j, c0_, pT_v, b_pT = item
                for cc in range(2):
                    k.op("pe", [b_ckv_tok, b_pT], [b_pb[cc]],
                         lambda e: e.matmul(pb[cc][:, c0_:T], lhsT=ckv_tok[:, j, cc * 128:(cc + 1) * 128], rhs=pT_v[:, c0_:T],
                                            start=(j == 0), stop=(j == nchunk - 1)))
                k.op("pe", [b_onesb, b_pT], [b_pb[2]],
                     lambda e: e.matmul(pb[2][:, c0_:T], lhsT=onesb[:], rhs=pT_v[:, c0_:T], start=(j == 0), stop=(j == nchunk - 1)))

            def stageC1(h):
                par = h % 2
                g_v, b_g = gth[par]
                k.op("act", [b_pb[2]], [b_ez0], lambda e: e.activation(out=ez0[:, 0:T], in_=pb[2][:], func=AF.Copy))
                for cc in range(2):
                    k.op("act", [b_pb[cc]], [b_olat], lambda e: e.activation(out=olat[:, cc, :], in_=pb[cc][:], func=AF.Copy))
                k.op("dve", [b_ez0], [b_rden], lambda e: e.reciprocal(out=rden[:, 0:T], in_=ez0[:, 0:T]))
                k.op("dve", [b_rden, b_g], [b_rden], lambda e: e.tensor_tensor(out=rden[:, 0:T], in0=rden[:, 0:T], in1=g_v[:, 0:T], op=ALU.mult))

            def stageC2(h):
                par = h % 2
                bo, b_bo = bank()
                for cc in range(2):
                    k.op("pe", [b_wkv[par], b_olat], [b_bo],
                         lambda e: e.matmul(bo[:], lhsT=wkv[par][:, 256 + cc * 128:256 + (cc + 1) * 128], rhs=olat[:, cc, :], start=(cc == 0), stop=(cc == 1)))
                k.op("dve", [b_bo, b_rden], [b_yT[h % 16]], lambda e: e.tensor_tensor(out=yT[:, h % 16, :], in0=bo[:], in1=rden[:, 0:T], op=ALU.mult))
                if h % 16 == 15:
                    w_out_half("wo", h // 16)

            indexer(0)
            for qb in range(3):
                indexer(qb + 1, bisect_gen(qb))
                mask_tr(qb)
            stageA(0)
            for _ in bisect_gen(3):
                pass
            mask_tr(3)

            jmid = min(2, nchunk - 1)
            for h in range(32):
                pend = []
                for j in range(nchunk):
                    pend.append(logits(h, j))
                    if j == jmid:
                        if h > 0:
                            stageC2(h - 1)
                        if h + 1 < 32:
                            stageA(h + 1)
                        if h == 6 and not cast_state["late_done"]:
                            cast_state["late_done"] = True
                            emit_casts(late_casts, [qTh[(h + 1) % 2][1]])
                    if len(pend) > LA:
                        pv(pend.pop(0))
                while pend:
                    pv(pend.pop(0))
                stageC1(h)
            stageC2(31)

        def layer1(ti):
            norm_and_transpose([(xb1, b_xb1), (ost[0][0].bitcast(BF16)[:, 0:D], ost[0][1])], junk1, b_junk1, dg1, b_dg1, gb, b_gb)
            W = 16 + T
            def U(g):
                pooled, b_pooled = pooled2[g % 2]
                nst = g + 1
                win = 2 ** nst
                for cj in range(8):
                    ch = g * 8 + cj
                    wt, b_wt = load_w("wu", ch)
                    bk, b_bk = bank()
                    proj_fm(wt, b_wt, bk, b_bk)
                    ue_v, b_ue = ue[ch % 2]
                    if ti == 0:
                        k.op("pool", [], [b_ue], lambda e: e.memset(ue_v[:, 0:16], 0.0))
                    else:
                        k.op("pool", [b_halo], [b_ue], lambda e: e.tensor_copy(out=ue_v[:, 0:16], in_=halo[:, ch, :]))
                    k.op("dve", [b_bk, b_rbc], [b_ue], lambda e: e.tensor_tensor(out=ue_v[:, 16:W], in0=bk[:], in1=rbc[:], op=ALU.mult))
                    k.op("pool", [b_ue], [b_halo], lambda e: e.tensor_copy(out=halo[:, ch, :], in_=ue_v[:, T:W]))
                    cur, b_cur = ue_v, b_ue
                    for st in range(nst):
                        sh = 2 ** st
                        lo_ = 2 ** (st + 1)
                        dst, b_dst = (sB, b_sB) if st % 2 == 0 else (sC, b_sC)
                        k.op("pool", [b_cur], [b_dst],
                             lambda e: e.tensor_tensor(out=dst[:, lo_:W], in0=cur[:, lo_:W], in1=cur[:, lo_ - sh:W - sh], op=ALU.add))
                        cur, b_cur = dst, b_dst
                    k.op("dve", [b_cur, b_ue], [b_pooled],
                         lambda e: e.scalar_tensor_tensor(out=pooled[:, cj, :], in0=cur[:, 16:W], scalar=1.0 / win, in1=ue_v[:, 16:W], op0=ALU.mult, op1=ALU.subtract))
                    if ti == 0:
                        k.op("dve", [b_cur, b_icnt], [b_sml], lambda e: e.tensor_tensor(out=sml[:, 32:48], in0=cur[:, 16:32], in1=icnt[:, g * 16:(g + 1) * 16], op=ALU.mult))
                        k.op("dve", [b_sml, b_ue], [b_pooled], lambda e: e.tensor_tensor(out=pooled[:, cj, 0:16], in0=sml[:, 32:48], in1=ue_v[:, 16:32], op=ALU.subtract))

            def WZ(g):
                pooled, b_pooled = pooled2[g % 2]
                for qc in range(8):
                    ch = g * 8 + qc
                    i = ch % 2
                    k.dma("sp", wgb[i][:].rearrange("p c n -> p (c n)"), wdst["wg"][ch, :, :], [b_wd["wg"]], [b_wgb[i]], f"wg{i}")
                    bm, b_bm = bank()
                    for pc in range(8):
                        k.op("pe", [b_wgb[i], b_pooled], [b_bm], lambda e: e.matmul(bm[:], lhsT=wgb[i][:, pc, :], rhs=pooled[:, pc, :], start=(pc == 0), stop=(pc == 7)))
                    wt, b_wt = load_w("wz1", ch)
                    bz, b_bz = bank()
                    proj_fm(wt, b_wt, bz, b_bz)
                    silu_gate(bz, b_bz, zs1, b_zs1, ez1, b_ez1, gt1, b_gt1)
                    k.op("dve", [b_bm, b_bgrp, b_scb], [b_mx1],
                         lambda e: e.tensor_scalar(out=mx1[:, 0:T], in0=bm[:], scalar1=bgrp[:, ch:ch + 1], scalar2=scb[:, ch:ch + 1], op0=ALU.add, op1=ALU.mult))
                    k.op("dve", [b_mx1, b_gt1], [b_yT[ch % 16]], lambda e: e.tensor_tensor(out=yT[:, ch % 16, :], in0=mx1[:, 0:T], in1=gt1[:, 0:T], op=ALU.mult))
                if g % 2 == 1:
                    w_out_half("wo1", g // 2)


            U(0)
            for g in range(4):
                if g + 1 < 4:
                    U(g + 1)
                WZ(g)

        b_out = Buf("out")

        def final_norm_store(bi, ti, do_norm):
            if do_norm:
                k.dma("sp", gfin[:, 0:D], gfin_d[:, :], [], [b_gfin], "gfin")
            for tb in range(4):
                if do_norm:
                    k.op("act", [b_xs[tb]], [b_junk1, b_sml],
                         lambda e: e.activation(out=junk1[:, 0:D], in_=xs[:, tb, :], func=AF.Square, accum_out=sml[:, 20 + tb:21 + tb]))
                    k.op("act", [b_sml], [b_sml], lambda e: e.activation(out=sml[:, 24 + tb:25 + tb], in_=sml[:, 20 + tb:21 + tb], func=AF.Sqrt, scale=1.0 / D, bias=EPS))
                    k.op("dve", [b_sml], [b_sml], lambda e: e.reciprocal(out=sml[:, 28 + tb:29 + tb], in_=sml[:, 24 + tb:25 + tb]))
                    o_v, b_o = ost[tb % 2]
                    k.op("dve", [b_xs[tb], b_sml, b_gfin], [b_o],
                         lambda e: e.scalar_tensor_tensor(out=o_v[:, 0:D], in0=xs[:, tb, :], scalar=sml[:, 28 + tb:29 + tb], in1=gfin[:, 0:D], op0=ALU.mult, op1=ALU.mult))
                    r0 = ti * T + tb * 128
                    k.dma("sp", out_d[bi, r0:r0 + 128, :], o_v[:, 0:D], [b_o], [b_out], f"st{tb % 2}")
                else:
                    r0 = ti * T + tb * 128
                    k.dma("act", out_d[bi, r0:r0 + 128, :], xs[:, tb, :], [b_xs[tb]], [b_out], f"st{tb}")

        for bi in range(NB):
            for ti in range(NTILE):
                for tb in range(4):
                    r0 = ti * T + tb * 128
                    k.dma("act", xs[:, tb, :], x_d[bi, r0:r0 + 128, :], [], [b_xs[tb]], f"ld{tb}")
                if do_l0:
                    layer0(ti)
                    k.barrier(b_scr_all)
                if do_l1:
                    layer1(ti)
                final_norm_store(bi, ti, do_l1)
                if do_l1:
                    k.barrier(b_scr_all)
        fin_bufs = [b_out] + list(b_xs)
        if do_l1:
            fin_bufs += [b for (_, b) in ost]
        k.finish("sp", fin_bufs)
        k.finish("act", fin_bufs)
        print("instructions", k.nins, "waits", k.nwaits, {e: k.cnt[e] for e in k.cnt})
    return nc


def _t5_bucket_np(dist):
    d = np.maximum(dist, 0)
    df = np.maximum(d, 1).astype(np.float32)
    large = 16 + (np.log(df / 16) / np.log(128 / 16) * 16).astype(np.int32)
    large = np.minimum(large, 31)
    return np.where(d < 16, d, large)


def tile_cols(w, n):
    ncol = w.shape[1]
    nb = ncol // n
    return np.ascontiguousarray(w.reshape(16, 128, nb, n).transpose(2, 1, 0, 3)).reshape(nb, 128, 16 * n)


def tile_wout(w):
    return np.ascontiguousarray(w.reshape(2, 2, 8, 128, 4, 512).transpose(0, 4, 1, 3, 2, 5)).reshape(16, 128, 4096)


def host_layout(inp, do_l0=True, do_l1=True):
    f = lambda a: np.ascontiguousarray(np.asarray(a, dtype=np.float32))
    m = {}
    if do_l0:
        w_in = f(inp["w_in_a"][0])
        m["wq"] = tile_cols(w_in[:, 0:4096], 128)
        m["wz"] = tile_cols(w_in[:, 5456:9552], 128)
        m["wqi"] = tile_cols(w_in[:, 4352:5376], 128)
        wsm = np.concatenate([w_in[:, 4096:4352], w_in[:, 5376:5440], w_in[:, 5440:5456]], axis=1)
        m["wsm"] = tile_cols(wsm, 336)
        m["wuk"] = np.ascontiguousarray(f(inp["w_uk_a"][0]).transpose(1, 2, 0))
        m["wuv"] = np.ascontiguousarray(f(inp["w_uv_a"][0]).reshape(2, 128, 32, 128).transpose(2, 1, 0, 3)).reshape(32, 128, 256)
        m["wo"] = tile_wout(f(inp["w_out_a"][0]))
        rb = f(inp["rel_bias"])
        s = np.arange(128)[:, None]
        t = np.arange(128)[None, :]
        idx = np.concatenate([_t5_bucket_np(t - s), _t5_bucket_np(t - s + 128)], axis=1)
        m["bias"] = np.ascontiguousarray(rb[idx].transpose(2, 0, 1))
    if do_l1:
        w_in = f(inp["w_in_b"][0])
        m["wu"] = tile_cols(w_in[:, 0:4096], 128)
        m["wz1"] = tile_cols(w_in[:, 4096:8192], 128)
        wg = f(inp["w_grp_b"][0])
        m["wg"] = np.ascontiguousarray(wg.reshape(4, 8, 128, 8, 128).transpose(0, 3, 2, 1, 4)).reshape(32, 128, 1024)
        m["wo1"] = tile_wout(f(inp["w_out_b"][0]))
    colT = lambda v: np.ascontiguousarray(f(v).reshape(-1, 128).T)
    bc = lambda v: np.ascontiguousarray(np.broadcast_to(f(v).reshape(1, -1), (128, f(v).size)))
    m["ident"] = np.eye(128, dtype=np.float32)
    m["gaT"] = colT(inp["norm_a"][0])
    m["gbT"] = colT(inp["norm_b"][0])
    m["gkv"] = bc(inp["kv_norm_a"][0])
    m["gki"] = bc(inp["kidx_norm_a"][0])
    m["gfin"] = bc(inp["final_norm"])
    m["rb31"] = bc(f(inp["rel_bias"])[31])
    m["bgrp"] = colT(f(inp["b_grp_b"][0]).reshape(-1))
    m["scb"] = colT(inp["scale_b"][0])
    ss = np.arange(128)[None, :]
    tt = np.arange(128)[:, None]
    m["cneg"] = np.where(ss <= tt, 0.0, NEG).astype(np.float32)
    ic = np.zeros((128, 64), np.float32)
    for g in range(4):
        ic[:, g * 16:(g + 1) * 16] = 1.0 / np.minimum(np.arange(16) + 1, 2 ** (g + 1))
    m["icnt"] = ic
    m["cfs"] = np.ascontiguousarray(np.broadcast_to((0.5 ** (np.arange(32) + 1.0)).astype(np.float32)[None, :], (128, 32)))
    return m


_CACHE = {}


def kernel(**inputs):
    x = np.ascontiguousarray(np.asarray(inputs["x"], dtype=np.float32))
    B = x.shape[0]
    ncores = 8
    NB = B // ncores
    m = host_layout(inputs)
    if "nc" not in _CACHE:
        _CACHE["nc"] = build(NB)
    nc = _CACHE["nc"]
    in_maps = []
    for c in range(ncores):
        d = dict(m)
        d["x"] = x[c * NB:(c + 1) * NB]
        in_maps.append(d)
    res = run_bass_kernel_spmd(nc, in_maps, core_ids=list(range(ncores)))
    return np.concatenate([r["out"] for r in res.results], axis=0)
```

```python
import numpy as np
from contextlib import ExitStack
import concourse.bass as bass
import concourse.mybir as mybir
from concourse.bass_utils import run_bass_kernel_spmd

F32 = mybir.dt.float32
BF16 = mybir.dt.bfloat16
AF = mybir.ActivationFunctionType
ALU = mybir.AluOpType
AX = mybir.AxisListType

SAME_ENGINE_SYNC = True
NIT = 16
TOPK = 256
L = 2048
D = 2048
T = 512
NTILE = L // T
EPS = 1e-6
NEG = -1.0e30


class Buf:
    __slots__ = ("name", "w", "r", "psum")

    def __init__(self, name, psum=False):
        self.name = name
        self.w = None
        self.r = {}
        self.psum = psum


class KB:
    def __init__(self, nc, es):
        self.nc = nc
        self.es = es
        self.eng = dict(pe=nc.tensor, act=nc.scalar, dve=nc.vector, pool=nc.gpsimd, sp=nc.sync)
        self.sem = {k: es.enter_context(nc.semaphore("s_" + k)) for k in self.eng}
        self.cnt = {k: 0 for k in self.eng}
        self.seen = {k: {} for k in self.eng}
        self.dsems = {}
        self.nwaits = 0
        self.nins = 0

    def sb(self, name, shape, dt, es=None):
        return (es or self.es).enter_context(self.nc.sbuf_tensor("sb_" + name, list(shape), dt))

    def ps(self, name, shape, dt):
        return self.es.enter_context(self.nc.psum_tensor("ps_" + name, list(shape), dt))

    def need(self, e, tok):
        key, sem, val = tok
        if key == e and (e == "pe" or not SAME_ENGINE_SYNC):
            return
        if self.seen[e].get(key, 0) >= val:
            return
        self.eng[e].wait_ge(sem, val)
        self.seen[e][key] = val
        self.nwaits += 1

    def _deps(self, e, reads, writes):
        for b in reads:
            if b.w is not None:
                self.need(e, b.w)
            if b.psum:
                for kk, t in b.r.items():
                    if kk != e:
                        self.need(e, t)
        for b in writes:
            if b.w is not None:
                self.need(e, b.w)
            for t in b.r.values():
                self.need(e, t)

    def op(self, e, reads, writes, fn, nosame=False):
        if nosame:
            sv = self.seen[e].get(e, 0)
            self.seen[e][e] = 1 << 60
        self._deps(e, reads, writes)
        if nosame:
            self.seen[e][e] = sv
        ins = fn(self.eng[e])
        self.cnt[e] += 1
        ins.then_inc(self.sem[e], 1)
        tok = (e, self.sem[e], self.cnt[e])
        for b in reads:
            b.r[e] = tok
        for b in writes:
            b.w = tok
            b.r = {}
        self.nins += 1
        return ins

    def dma(self, q, out_ap, in_ap, reads, writes, dsem, **kw):
        key0 = "d:" + dsem
        for b in reads:
            if b.w is not None:
                self.need(q, b.w)
        for b in writes:
            if b.w is not None and b.w[0] != key0:
                self.need(q, b.w)
            for t in b.r.values():
                self.need(q, t)
        if dsem not in self.dsems:
            self.dsems[dsem] = [self.es.enter_context(self.nc.semaphore("d_" + dsem)), 0]
        ds = self.dsems[dsem]
        ins = self.eng[q].dma_start(out=out_ap, in_=in_ap, **kw)
        ds[1] += 16
        ins.then_inc(ds[0], 16)
        key = "d:" + dsem
        tok = (key, ds[0], ds[1])
        for b in reads:
            b.r[key] = tok
        for b in writes:
            b.w = tok
            b.r = {}
        self.nins += 1
        return ins

    def finish(self, e, bufs):
        for b in bufs:
            if b.w is not None:
                self.need(e, b.w)
            for t in b.r.values():
                self.need(e, t)

    def barrier(self, bufs):
        for e in ("pe", "act", "dve", "pool", "sp"):
            self.finish(e, bufs)


WSPEC_L0 = dict(wq=(32, 2048), wz=(32, 2048), wqi=(8, 2048), wsm=(1, 16 * 336), wuk=(32, 256), wuv=(32, 256),
                wo=(16, 4096), bias=(32, 256))
WSPEC_L1 = dict(wu=(32, 2048), wz1=(32, 2048), wg=(32, 1024), wo1=(16, 4096))
GFOLD = dict(wq="ga", wz="ga", wqi="ga", wsm="ga", wu="gb", wz1="gb")
GFOLD_N = dict(wq=128, wz=128, wqi=128, wsm=336, wu=128, wz1=128)


def build(NB, do_l0=True, do_l1=True):
    nc = bass.Bass("TRN2", target_bir_lowering=False)
    dram = lambda n, s, kind="ExternalInput", dt=F32: nc.dram_tensor(n, list(s), dt, kind=kind).ap()
    x_d = dram("x", [NB, L, D])
    out_d = dram("out", [NB, L, D], kind="ExternalOutput")
    wspec = {}
    if do_l0:
        wspec.update(WSPEC_L0)
    if do_l1:
        wspec.update(WSPEC_L1)
    wsrc = {n: dram(n, [nb, 128, F]) for n, (nb, F) in wspec.items()}
    wdst = {n: dram(n + "_b", [nb, 128, F], kind="Internal", dt=BF16) for n, (nb, F) in wspec.items()}
    ident_d = dram("ident", [128, 128])
    gaT_d = dram("gaT", [128, 16])
    gbT_d = dram("gbT", [128, 16])
    gkv_d = dram("gkv", [128, 256])
    gki_d = dram("gki", [128, 64])
    gfin_d = dram("gfin", [128, 2048])
    rb31_d = dram("rb31", [128, 32])
    bgrp_d = dram("bgrp", [128, 32])
    scb_d = dram("scb", [128, 32])
    cneg_d = dram("cneg", [128, 128])
    icnt_d = dram("icnt", [128, 64])
    cfs_d = dram("cfs", [128, 32])

    with ExitStack() as es:
        k = KB(nc, es)

        def const(name, src, shape):
            t = k.sb(name, shape, F32)
            b = Buf(name)
            k.dma("sp", t[:], src, [], [b], "c_" + name)
            return t, b

        idf, b_idf = const("idf", ident_d[:, :], [128, 128])
        ga, b_ga = const("ga", gaT_d[:, :], [128, 16])
        gb, b_gb = const("gb", gbT_d[:, :], [128, 16])
        gkv, b_gkv = const("gkv", gkv_d[:, :], [128, 256])
        gki, b_gki = const("gki", gki_d[:, :], [128, 64])
        rb31, b_rb31 = const("rb31", rb31_d[:, :], [128, 32])
        bgrp, b_bgrp = const("bgrp", bgrp_d[:, :], [128, 32])
        scb, b_scb = const("scb", scb_d[:, :], [128, 32])
        cneg, b_cneg = const("cneg", cneg_d[:, :], [128, 128])
        icnt, b_icnt = const("icnt", icnt_d[:, :], [128, 64])
        cfs, b_cfs = const("cfs", cfs_d[:, :], [128, 32])
        idb = k.sb("idb", [128, 128], BF16); b_idb = Buf("idb")
        onesb = k.sb("onesb", [128, 128], BF16); b_onesb = Buf("onesb")
        onesf = k.sb("onesf", [128, 128], F32); b_onesf = Buf("onesf")
        k.op("dve", [b_idf], [b_idb], lambda e: e.tensor_copy(out=idb[:], in_=idf[:]))
        k.op("dve", [], [b_onesb], lambda e: e.memset(onesb[:], 1.0))
        k.op("dve", [], [b_onesf], lambda e: e.memset(onesf[:], 1.0))

        b_wd = {n: Buf("wd_" + n) for n in wspec}
        cast_order = [n for n in ("wsm", "wqi", "wuk", "wuv", "wq", "wz", "wo", "wu", "wg", "wz1", "wo1") if n in wspec]
        for n in cast_order:
            nb, F = wspec[n]
            rows = nb * 128
            nsplit = max(1, (rows * F * 4) // (8 << 20))
            while rows % nsplit:
                nsplit += 1
            rstep = rows // nsplit
            srcv = wsrc[n].rearrange("b p f -> (b p) f")
            dstv = wdst[n].rearrange("b p f -> (b p) f")
            for r0 in range(0, rows, rstep):
                k.dma("pool", dstv[r0:r0 + rstep, :], srcv[r0:r0 + rstep, :], [], [b_wd[n]], "cw_" + n, max_dma_last_dim=4096)

        xs = k.sb("xs", [128, 4, D], F32); b_xs = [Buf(f"xs{i}") for i in range(4)]
        xT = k.sb("xT", [128, 16, T], BF16); b_xT = Buf("xT")
        yT = k.sb("yT", [128, 16, T], BF16); b_yT = [Buf(f"yT{i}") for i in range(16)]
        rstd = k.sb("rstd", [128, 4], F32); b_rstd = Buf("rstd")
        rbc = k.sb("rbc", [128, T], F32); b_rbc = Buf("rbc")
        sml = k.sb("sml", [128, 64], F32); b_sml = Buf("sml")
        NWB = 3
        wbuf = [k.sb(f"wbuf{i}", [128, 16, 128], BF16) for i in range(NWB)]; b_wbuf = [Buf(f"wbuf{i}") for i in range(NWB)]
        wob = [k.sb(f"wob{i}", [128, 8, 512], BF16) for i in range(2)]; b_wob = [Buf(f"wob{i}") for i in range(2)]
        wrr = [0, 0]
        pb = [k.ps(f"pb{i}", [128, 512], F32) for i in range(8)]; b_pb = [Buf(f"pb{i}", psum=True) for i in range(8)]
        prr = [0]
        ptr = [0, 0]

        def bank():
            i = 3 + prr[0] % 5
            prr[0] += 1
            return pb[i], b_pb[i]

        def load_w(name, blk):
            i = wrr[0] % NWB
            wrr[0] += 1
            k.dma("sp", wbuf[i][:].rearrange("p c n -> p (c n)"), wdst[name][blk, :, :], [b_wd[name]], [b_wbuf[i]], f"wb{i}")
            return wbuf[i], b_wbuf[i]

        SCR = 67 * 1024
        scr = k.sb("scr", [128, SCR // 2], BF16)
        b_scr_all = []

        class Carver:
            def __init__(self):
                self.off = 0
                self.bufs = []

            def get(self, name, cols, dt):
                nbytes = cols * (4 if dt == F32 else 2)
                nbytes = (nbytes + 31) // 32 * 32
                assert self.off + nbytes <= SCR, (name, self.off, nbytes)
                v = scr[:, self.off // 2:(self.off + nbytes) // 2]
                self.off += nbytes
                if dt == F32:
                    v = v.bitcast(F32)
                v = v[:, 0:cols]
                b = Buf(name)
                self.bufs.append(b)
                return v, b

        if do_l0:
            ckv_tok = k.sb("ckv_tok", [128, 16, 256], BF16); b_ckv_tok = Buf("ckv_tok")
            ckvT = k.sb("ckvT", [128, 2, L], BF16); b_ckvT = Buf("ckvT")
            kidxT = k.sb("kidxT", [128, L], BF16); b_kidxT = Buf("kidxT")
            wsm = k.sb("wsm", [128, 16, 336], BF16); b_wsm = Buf("wsm")
            k.dma("sp", wsm[:].rearrange("p c n -> p (c n)"), wdst["wsm"][0, :, :], [b_wd["wsm"]], [b_wsm], "wsm")
            wkv = [k.sb(f"wkv{i}", [128, 512], BF16) for i in range(2)]; b_wkv = [Buf(f"wkv{i}") for i in range(2)]
            bsb = [k.sb(f"bsb{i}", [128, 256], BF16) for i in range(2)]; b_bsb = [Buf(f"bsb{i}") for i in range(2)]
            bsf = [k.sb(f"bsf{i}", [128, 256], F32) for i in range(2)]; b_bsf = [Buf(f"bsf{i}") for i in range(2)]
            c0 = Carver()
            maskT, b_maskT = c0.get("maskT", 16 * T, BF16)
            maskT = maskT.rearrange("p (j t) -> p j t", j=16)
            accraw, b_acc = c0.get("acc", L, F32)
            acc1, b_acc1 = c0.get("acc1", L, F32)
            acc = accraw
            xb0 = accraw.bitcast(BF16)[:, 0:D]; b_xb0 = b_acc
            msk, b_msk = c0.get("msk", L, BF16)
            junk0, b_junk0 = msk, b_msk
            qiT_flat, b_qiT = c0.get("qiT", 8 * T, BF16)
            qiT = qiT_flat.rearrange("p (j t) -> p j t", j=8)
            ktok, b_ktok = c0.get("ktok", 128, BF16)
            wts, b_wts = c0.get("wts", 64, F32)
            bis, b_bis = c0.get("bis", 16 + 64, F32)
            qTh = []; gth = []; qlT = []; pTs = []
            for i in range(2):
                qTh.append(c0.get(f"qTh{i}", T, BF16))
                gth.append(c0.get(f"gth{i}", T, BF16))
                v, b = c0.get(f"qlT{i}", 2 * T, BF16)
                qlT.append((v.rearrange("p (c t) -> p c t", c=2), b))
            zs0, b_zs0 = c0.get("zs0", T, F32)
            ez0, b_ez0 = c0.get("ez0", T, F32)
            for i in range(4):
                pTs.append(c0.get(f"pT{i}", T, BF16))
            rden, b_rden = c0.get("rden", T, F32)
            olat, b_olat = c0.get("olat", 2 * T, BF16)
            olat = olat.rearrange("p (c t) -> p c t", c=2)
            dg0, b_dg0 = c0.get("dg0", 128, F32)
            b_scr_all += c0.bufs
            print("scratch L0 bytes", c0.off)
        if do_l1:
            c1 = Carver()
            pooled2 = []
            for i in range(2):
                v, b = c1.get(f"pooled{i}", 8 * T, BF16)
                pooled2.append((v.rearrange("p (c t) -> p c t", c=8), b))
            ue = [c1.get(f"ue{i}", 16 + T, F32) for i in range(2)]
            sB, b_sB = c1.get("sB", 16 + T, F32)
            sC, b_sC = c1.get("sC", 16 + T, F32)
            zs1, b_zs1 = c1.get("zs1", T, F32)
            ez1, b_ez1 = c1.get("ez1", T, F32)
            gt1, b_gt1 = c1.get("gt1", T, F32)
            mx1, b_mx1 = c1.get("mx1", T, F32)
            xb1, b_xb1 = c1.get("xb1", D, BF16)
            junk1, b_junk1 = c1.get("junk1", D, BF16)
            dg1, b_dg1 = c1.get("dg1", 128, F32)
            gfin, b_gfin = c1.get("gfin", D, F32)
            ost = [c1.get(f"ost{i}", D, F32) for i in range(2)]
            halo = k.sb("halo", [128, 32, 16], F32); b_halo = Buf("halo")
            wgb = [k.sb(f"wgb{i}", [128, 8, 128], BF16) for i in range(2)]; b_wgb = [Buf(f"wgb{i}") for i in range(2)]
            b_scr_all += c1.bufs
            print("scratch L1 bytes", c1.off)

        def norm_and_transpose(xbs, junk_v, b_junk, dg_v, b_dg, gt, b_gt):
            for tb in range(4):
                xb_v, b_xb = xbs[tb % len(xbs)]
                k.op("act", [b_xs[tb]], [b_junk, b_sml],
                     lambda e: e.activation(out=junk_v[:, 0:D], in_=xs[:, tb, :], func=AF.Square, accum_out=sml[:, tb:tb + 1]))
                k.op("act", [b_sml], [b_sml],
                     lambda e: e.activation(out=sml[:, 4 + tb:5 + tb], in_=sml[:, tb:tb + 1], func=AF.Sqrt, scale=1.0 / D, bias=EPS))
                k.op("dve", [b_sml], [b_rstd], lambda e: e.reciprocal(out=rstd[:, tb:tb + 1], in_=sml[:, 4 + tb:5 + tb]))
                k.op("dve", [b_xs[tb]], [b_xb], lambda e: e.tensor_copy(out=xb_v[:, 0:D], in_=xs[:, tb, :]))
                for half in range(2):
                    bk, b_bk = bank()
                    bkb = bk.bitcast(BF16)
                    for j in range(8):
                        c = half * 8 + j
                        k.op("pe", [b_xb, b_idb], [b_bk],
                             lambda e: e.transpose(out=bkb[:, j * 128:(j + 1) * 128], in_=xb_v[:, c * 128:(c + 1) * 128], identity=idb[:]))
                    for j in range(8):
                        c = half * 8 + j
                        dst = xT[:, c, tb * 128:(tb + 1) * 128]
                        srcj = bkb[:, j * 128:(j + 1) * 128]
                        if j % 2 == 0:
                            k.op("act", [b_bk, b_gt], [b_xT], lambda e: e.activation(out=dst, in_=srcj, func=AF.Copy, scale=gt[:, c:c + 1]), nosame=(j > 0))
                        else:
                            k.op("dve", [b_bk, b_gt], [b_xT],
                                 lambda e: e.tensor_scalar(out=dst, in0=srcj, scalar1=gt[:, c:c + 1], scalar2=None, op0=ALU.mult), nosame=(j > 1))
            bk_r, b_bk_r = bank()
            for tb in range(4):
                k.op("dve", [b_idf, b_rstd], [b_dg],
                     lambda e: e.tensor_scalar(out=dg_v[:, 0:128], in0=idf[:], scalar1=rstd[:, tb:tb + 1], scalar2=None, op0=ALU.mult))
                k.op("pe", [b_dg, b_onesf], [b_bk_r],
                     lambda e: e.matmul(bk_r[:, tb * 128:(tb + 1) * 128], lhsT=onesf[:], rhs=dg_v[:, 0:128], start=True, stop=True))
            k.op("act", [b_bk_r], [b_rbc], lambda e: e.activation(out=rbc[:], in_=bk_r[:], func=AF.Copy))

        def proj_fm(wt, b_wt, bk, b_bk):
            for c in range(16):
                k.op("pe", [b_wt, b_xT], [b_bk], lambda e: e.matmul(bk[:], lhsT=wt[:, c, :], rhs=xT[:, c, :], start=(c == 0), stop=(c == 15)))

        def silu_gate(bk, b_bk, zs_v, b_zs, ez_v, b_ez, out_v, b_out):
            k.op("dve", [b_bk, b_rbc], [b_zs],
                 lambda e: e.scalar_tensor_tensor(out=zs_v[:, 0:T], in0=bk[:], scalar=0.5, in1=rbc[:], op0=ALU.mult, op1=ALU.mult))
            k.op("act", [b_zs], [b_ez], lambda e: e.activation(out=ez_v[:, 0:T], in_=zs_v[:, 0:T], func=AF.Tanh))
            k.op("dve", [b_ez, b_zs], [b_out],
                 lambda e: e.scalar_tensor_tensor(out=out_v[:, 0:T], in0=ez_v[:, 0:T], scalar=1.0, in1=zs_v[:, 0:T], op0=ALU.add, op1=ALU.mult))

        def w_out_half(wname, hf):
            for dmc in range(4):
                accb = [bank() for _ in range(4)]
                for gr in range(2):
                    i = wrr[1] % 2
                    wrr[1] += 1
                    blk = (hf * 4 + dmc) * 2 + gr
                    k.dma("sp", wob[i][:].rearrange("p c n -> p (c n)"), wdst[wname][blk, :, :], [b_wd[wname]], [b_wob[i]], f"wo{i}")
                    for tb in range(4):
                        for ic in range(8):
                            ci = gr * 8 + ic
                            k.op("pe", [b_wob[i], b_yT[ci]], [accb[tb][1]],
                                 lambda e: e.matmul(accb[tb][0][:], lhsT=yT[:, ci, tb * 128:(tb + 1) * 128], rhs=wob[i][:, ic, :],
                                                    start=(ci == 0), stop=(ci == 15)))
                for tb in range(4):
                    k.op("dve", [accb[tb][1], b_xs[tb]], [b_xs[tb]],
                         lambda e: e.tensor_tensor(out=xs[:, tb, dmc * 512:(dmc + 1) * 512], in0=accb[tb][0][:],
                                                   in1=xs[:, tb, dmc * 512:(dmc + 1) * 512], op=ALU.add))

        def layer0(ti):
            norm_and_transpose([(xb0, b_xb0), (acc1.bitcast(BF16)[:, 0:D], b_acc1)], junk0, b_junk0, dg0, b_dg0, ga, b_ga)
            for tb in range(4):
                gbk = ti * 4 + tb
                bk, b_bk = bank()
                for c in range(16):
                    k.op("pe", [b_xT, b_wsm], [b_bk],
                         lambda e: e.matmul(bk[:, 0:336], lhsT=xT[:, c, tb * 128:(tb + 1) * 128], rhs=wsm[:, c, :], start=(c == 0), stop=(c == 15)))
                k.op("act", [b_bk], [b_junk0, b_sml],
                     lambda e: e.activation(out=junk0[:, 0:256], in_=bk[:, 0:256], func=AF.Square, accum_out=sml[:, 8:9]))
                k.op("act", [b_bk], [b_junk0, b_sml],
                     lambda e: e.activation(out=junk0[:, 256:320], in_=bk[:, 256:320], func=AF.Square, accum_out=sml[:, 9:10]))
                k.op("dve", [b_rstd], [b_sml], lambda e: e.tensor_tensor(out=sml[:, 10:11], in0=rstd[:, tb:tb + 1], in1=rstd[:, tb:tb + 1], op=ALU.mult))
                k.op("dve", [b_sml], [b_sml], lambda e: e.tensor_scalar(out=sml[:, 11:13], in0=sml[:, 8:10], scalar1=sml[:, 10:11], scalar2=None, op0=ALU.mult))
                k.op("act", [b_sml], [b_sml], lambda e: e.activation(out=sml[:, 13:14], in_=sml[:, 11:12], func=AF.Sqrt, scale=1.0 / 256, bias=EPS))
                k.op("act", [b_sml], [b_sml], lambda e: e.activation(out=sml[:, 14:15], in_=sml[:, 12:13], func=AF.Sqrt, scale=1.0 / 64, bias=EPS))
                k.op("dve", [b_sml], [b_sml], lambda e: e.reciprocal(out=sml[:, 15:17], in_=sml[:, 13:15]))
                k.op("dve", [b_sml, b_rstd], [b_sml], lambda e: e.tensor_scalar(out=sml[:, 17:19], in0=sml[:, 15:17], scalar1=rstd[:, tb:tb + 1], scalar2=None, op0=ALU.mult))
                k.op("dve", [b_bk, b_sml, b_gkv], [b_ckv_tok],
                     lambda e: e.scalar_tensor_tensor(out=ckv_tok[:, gbk, :], in0=bk[:, 0:256], scalar=sml[:, 17:18], in1=gkv[:], op0=ALU.mult, op1=ALU.mult))
                for rep in range(2):
                    k.op("dve", [b_bk, b_sml, b_gki], [b_ktok],
                         lambda e: e.scalar_tensor_tensor(out=ktok[:, rep * 64:(rep + 1) * 64], in0=bk[:, 256:320], scalar=sml[:, 18:19], in1=gki[:], op0=ALU.mult, op1=ALU.mult))
                k.op("act", [b_bk], [b_wts], lambda e: e.activation(out=wts[:, tb * 16:(tb + 1) * 16], in_=bk[:, 320:336], func=AF.Copy))
                bk2, b_bk2 = bank()
                bk2b = bk2.bitcast(BF16)
                for cc in range(2):
                    k.op("pe", [b_ckv_tok, b_idb], [b_bk2],
                         lambda e: e.transpose(out=bk2b[:, cc * 128:(cc + 1) * 128], in_=ckv_tok[:, gbk, cc * 128:(cc + 1) * 128], identity=idb[:]))
                k.op("pe", [b_ktok, b_idb], [b_bk2], lambda e: e.transpose(out=bk2b[:, 256:384], in_=ktok[:, 0:128], identity=idb[:]))
                k.op("act", [b_bk2], [b_ckvT],
                     lambda e: e.activation(out=ckvT[:, :, gbk * 128:(gbk + 1) * 128], in_=bk2b[:, 0:256].rearrange("p (c t) -> p c t", c=2), func=AF.Copy))
                k.op("dve", [b_bk2], [b_kidxT], lambda e: e.tensor_copy(out=kidxT[:, gbk * 128:(gbk + 1) * 128], in_=bk2b[:, 256:384]))
            for j in range(8):
                wt, b_wt = load_w("wqi", j)
                bk, b_bk = bank()
                proj_fm(wt, b_wt, bk, b_bk)
                if j % 2 == 0:
                    k.op("act", [b_bk], [b_qiT], lambda e: e.activation(out=qiT[:, j, :], in_=bk[:], func=AF.Copy))
                else:
                    k.op("dve", [b_bk], [b_qiT], lambda e: e.tensor_copy(out=qiT[:, j, :], in_=bk[:]))
            accs = [(acc, b_acc), (acc1, b_acc1)]
            tmps = [(zs0, b_zs0), (ez0, b_ez0), (rden, b_rden)]

            def indexer(qb, filler=None):
                gbk = ti * 4 + qb
                S = (gbk + 1) * 128
                a_v, b_a = accs[qb % 2]
                bo_ = 8 * (qb % 2)
                nparts_tot = 16 * ((S + 511) // 512)
                stride = max(1, nparts_tot // (NIT + 2))
                pcount = [0]
                for h in range(16):
                    pr = 64 * (h % 2)
                    for s0 in range(0, S, 512):
                        n = min(512, S - s0)
                        bk, b_bk = bank()
                        k.op("pe", [b_qiT, b_kidxT], [b_bk],
                             lambda e: e.matmul(bk[:, 0:n], lhsT=qiT[pr:pr + 64, h // 2, qb * 128:(qb + 1) * 128], rhs=kidxT[pr:pr + 64, s0:s0 + n],
                                                start=True, stop=True))
                        wcol = wts[:, qb * 16 + h:qb * 16 + h + 1]
                        if h == 0:
                            k.op("dve", [b_bk, b_wts], [b_a],
                                 lambda e: e.tensor_scalar(out=a_v[:, s0:s0 + n], in0=bk[:, 0:n], scalar1=0.0, scalar2=wcol, op0=ALU.max, op1=ALU.mult))
                        elif h % 4 == 2:
                            t_v, b_t = tmps[ptr[1] % 3]
                            ptr[1] += 1
                            k.op("dve", [b_bk, b_wts], [b_t],
                                 lambda e: e.tensor_scalar(out=t_v[:, 0:n], in0=bk[:, 0:n], scalar1=0.0, scalar2=wcol, op0=ALU.max, op1=ALU.mult))
                            k.op("dve", [b_t, b_a], [b_a],
                                 lambda e: e.tensor_tensor(out=a_v[:, s0:s0 + n], in0=a_v[:, s0:s0 + n], in1=t_v[:, 0:n], op=ALU.add))
                        else:
                            k.op("act", [b_bk], [b_bk], lambda e: e.activation(out=bk[:, 0:n], in_=bk[:, 0:n], func=AF.Relu))
                            k.op("dve", [b_bk, b_wts, b_a], [b_a],
                                 lambda e: e.scalar_tensor_tensor(out=a_v[:, s0:s0 + n], in0=bk[:, 0:n], scalar=wcol, in1=a_v[:, s0:s0 + n],
                                                                  op0=ALU.mult, op1=ALU.add))
                        pcount[0] += 1
                        if filler is not None and pcount[0] % stride == 0:
                            next(filler, None)
                if filler is not None:
                    for _ in filler:
                        pass
                k.op("dve", [b_a], [b_bis], lambda e: e.tensor_reduce(out=bis[:, bo_:bo_ + 1], in_=a_v[:, 0:S], axis=AX.X, op=ALU.max, apply_absolute_value=True))
                k.op("dve", [b_bis], [b_bis], lambda e: e.tensor_scalar(out=bis[:, bo_ + 1:bo_ + 2], in0=bis[:, bo_:bo_ + 1], scalar1=2.0, scalar2=1.0, op0=ALU.mult, op1=ALU.add))
                nd0 = 16 + 32 * (qb % 2)
                k.op("dve", [b_bis, b_cfs], [b_bis],
                     lambda e: e.tensor_scalar(out=bis[:, nd0:nd0 + 32], in0=cfs[:, 0:32], scalar1=bis[:, bo_ + 1:bo_ + 2], scalar2=-1.0, op0=ALU.mult, op1=ALU.mult))
                k.op("dve", [b_a, b_cneg], [b_a],
                     lambda e: e.tensor_tensor(out=a_v[:, gbk * 128:(gbk + 1) * 128], in0=a_v[:, gbk * 128:(gbk + 1) * 128], in1=cneg[:], op=ALU.add))

            def bisect_gen(qb):
                gbk = ti * 4 + qb
                S = (gbk + 1) * 128
                a_v, b_a = accs[qb % 2]
                bo_ = 8 * (qb % 2)
                nd0 = 16 + 32 * (qb % 2)
                nm = [bis[:, bo_ + 2:bo_ + 3], bis[:, bo_ + 3:bo_ + 4]]
                sg = bis[:, bo_ + 4:bo_ + 5]
                s1 = bis[:, bo_ + 5:bo_ + 6]
                thr = bis[:, bo_ + 6:bo_ + 7]
                if S > TOPK:
                    k.op("act", [], [b_bis], lambda e: e.activation(out=nm[0], in_=cfs[:, 0:1], func=AF.Copy))
                    for it in range(NIT):
                        cur = nm[it % 2]
                        nxt = nm[(it + 1) % 2]
                        k.op("act", [b_a, b_bis], [b_msk, b_bis],
                             lambda e: e.activation(out=msk[:, 0:S], in_=a_v[:, 0:S], func=AF.Sign, bias=cur, accum_out=sg))
                        k.op("act", [b_bis], [b_bis], lambda e: e.activation(out=s1, in_=sg, func=AF.Sign, bias=float(S - 2 * TOPK + 1)))
                        k.op("act", [b_bis], [b_bis],
                             lambda e: e.activation(out=nxt, in_=s1, func=AF.Identity, scale=bis[:, nd0 + it + 1:nd0 + it + 2], bias=cur))
                        yield
                    fin = nm[NIT % 2]
                    k.op("dve", [b_bis], [b_bis],
                         lambda e: e.tensor_scalar(out=thr, in0=fin, scalar1=-1.0, scalar2=bis[:, nd0 + NIT:nd0 + NIT + 1], op0=ALU.mult, op1=ALU.add))
                else:
                    k.op("dve", [b_bis], [b_bis], lambda e: e.tensor_scalar(out=thr, in0=bis[:, bo_:bo_ + 1], scalar1=-1.0, scalar2=-1.0, op0=ALU.mult, op1=ALU.add))

            def mask_tr(qb):
                gbk = ti * 4 + qb
                S = (gbk + 1) * 128
                a_v, b_a = accs[qb % 2]
                bo_ = 8 * (qb % 2)
                thr = bis[:, bo_ + 6:bo_ + 7]
                k.op("dve", [b_a, b_bis], [b_msk],
                     lambda e: e.tensor_scalar(out=msk[:, 0:S], in0=a_v[:, 0:S], scalar1=thr, scalar2=None, op0=ALU.is_gt))
                for j0 in range(0, gbk + 1, 8):
                    nj = min(8, gbk + 1 - j0)
                    bk, b_bk = bank()
                    bkb = bk.bitcast(BF16)
                    for jj in range(nj):
                        j = j0 + jj
                        k.op("pe", [b_msk, b_idb], [b_bk],
                             lambda e: e.transpose(out=bkb[:, jj * 128:(jj + 1) * 128], in_=msk[:, j * 128:(j + 1) * 128], identity=idb[:]))
                    k.op("act", [b_bk], [b_maskT],
                         lambda e: e.activation(out=maskT[:, j0:j0 + nj, qb * 128:(qb + 1) * 128],
                                                in_=bkb[:, 0:nj * 128].rearrange("p (j t) -> p j t", j=nj), func=AF.Copy))

            nchunk = ti * 4 + 4
            sc = 128.0 ** -0.5
            LA = 3

            def stageA(h):
                par = h % 2
                wq_t, b_wq = load_w("wq", h)
                wz_t, b_wz = load_w("wz", h)
                k.dma("sp", wkv[par][:, 0:256], wdst["wuk"][h, :, :], [b_wd["wuk"]], [b_wkv[par]], f"wkv{par}")
                k.dma("sp", wkv[par][:, 256:512], wdst["wuv"][h, :, :], [b_wd["wuv"]], [b_wkv[par]], f"wkv{par}")
                k.dma("sp", bsf[par][:, :], wsrc["bias"][h, :, :], [], [b_bsf[par]], f"bsf{par}")
                k.op("dve", [b_bsf[par], b_rb31], [b_bsb[par]],
                     lambda e: e.tensor_scalar(out=bsb[par][:, :], in0=bsf[par][:, :], scalar1=rb31[:, h:h + 1], scalar2=float(128.0 ** 0.5),
                                               op0=ALU.subtract, op1=ALU.mult))
                qT_v, b_qT = qTh[par]
                g_v, b_g = gth[par]
                ql_v, b_ql = qlT[par]
                bq, b_bq = bank()
                proj_fm(wq_t, b_wq, bq, b_bq)
                k.op("dve", [b_bq, b_rbc], [b_qT], lambda e: e.tensor_tensor(out=qT_v[:, 0:T], in0=bq[:], in1=rbc[:], op=ALU.mult))
                bz, b_bz = bank()
                proj_fm(wz_t, b_wz, bz, b_bz)
                silu_gate(bz, b_bz, zs0, b_zs0, ez0, b_ez0, g_v, b_g)
                for cc in range(2):
                    bl, b_bl = bank()
                    k.op("pe", [b_wkv[par], b_qT], [b_bl],
                         lambda e: e.matmul(bl[:], lhsT=wkv[par][:, cc * 128:(cc + 1) * 128], rhs=qT_v[:, 0:T], start=True, stop=True))
                    k.op("act", [b_bl], [b_ql], lambda e: e.activation(out=ql_v[:, cc, :], in_=bl[:], func=AF.Copy))

            def logits(h, j):
                par = h % 2
                ql_v, b_ql = qlT[par]
                jl = j - ti * 4
                c0_ = max(0, jl) * 128
                lg, b_lg = bank()
                near = []
                for qbl in range(4):
                    dist = (ti * 4 + qbl) - j
                    if dist in (0, 1):
                        near.append((qbl, dist))
                for cc in range(2):
                    k.op("pe", [b_ckvT, b_ql], [b_lg],
                         lambda e: e.matmul(lg[:, c0_:T], lhsT=ckvT[:, cc, j * 128:(j + 1) * 128], rhs=ql_v[:, cc, c0_:T],
                                            start=(cc == 0), stop=(cc == 1 and not near)))
                for ni, (qbl, dist) in enumerate(near):
                    k.op("pe", [b_idb, b_bsb[par]], [b_lg],
                         lambda e: e.matmul(lg[:, qbl * 128:(qbl + 1) * 128], lhsT=idb[:], rhs=bsb[par][:, dist * 128:(dist + 1) * 128],
                                            start=False, stop=(ni == len(near) - 1)))
                pT_v, b_pT = pTs[ptr[0] % 4]
                ptr[0] += 1
                k.op("act", [b_lg, b_rb31], [b_pT],
                     lambda e: e.activation(out=pT_v[:, c0_:T], in_=lg[:, c0_:T], func=AF.Exp, scale=sc, bias=rb31[:, h:h + 1]))
                k.op("dve", [b_pT, b_maskT], [b_pT],
                     lambda e: e.tensor_tensor(out=pT_v[:, c0_:T], in0=pT_v[:, c0_:T], in1=maskT[:, j, c0_:T], op=ALU.mult))
                return (j, c0_, pT_v, b_pT)

            def pv(item):
                j, c0_, pT_v, b_pT = item
                for cc in range(2):
                    k.op("pe", [b_ckv_tok, b_pT], [b_pb[cc]],
                         lambda e: e.matmul(pb[cc][:, c0_:T], lhsT=ckv_tok[:, j, cc * 128:(cc + 1) * 128], rhs=pT_v[:, c0_:T],
                                            start=(j == 0), stop=(j == nchunk - 1)))
                k.op("pe", [b_onesb, b_pT], [b_pb[2]],
                     lambda e: e.matmul(pb[2][:, c0_:T], lhsT=onesb[:], rhs=pT_v[:, c0_:T], start=(j == 0), stop=(j == nchunk - 1)))

            def stageC1(h):
                par = h % 2
                g_v, b_g = gth[par]
                k.op("act", [b_pb[2]], [b_ez0], lambda e: e.activation(out=ez0[:, 0:T], in_=pb[2][:], func=AF.Copy))
                for cc in range(2):
                    k.op("act", [b_pb[cc]], [b_olat], lambda e: e.activation(out=olat[:, cc, :], in_=pb[cc][:], func=AF.Copy))
                k.op("dve", [b_ez0], [b_rden], lambda e: e.reciprocal(out=rden[:, 0:T], in_=ez0[:, 0:T]))
                k.op("dve", [b_rden, b_g], [b_rden], lambda e: e.tensor_tensor(out=rden[:, 0:T], in0=rden[:, 0:T], in1=g_v[:, 0:T], op=ALU.mult))

            def stageC2(h):
                par = h % 2
                bo, b_bo = bank()
                for cc in range(2):
                    k.op("pe", [b_wkv[par], b_olat], [b_bo],
                         lambda e: e.matmul(bo[:], lhsT=wkv[par][:, 256 + cc * 128:256 + (cc + 1) * 128], rhs=olat[:, cc, :], start=(cc == 0), stop=(cc == 1)))
                k.op("dve", [b_bo, b_rden], [b_yT[h % 16]], lambda e: e.tensor_tensor(out=yT[:, h % 16, :], in0=bo[:], in1=rden[:, 0:T], op=ALU.mult))
                if h % 16 == 15:
                    w_out_half("wo", h // 16)

            indexer(0)
            for qb in range(3):
                indexer(qb + 1, bisect_gen(qb))
                mask_tr(qb)
            stageA(0)
            for _ in bisect_gen(3):
                pass
            mask_tr(3)

            jmid = min(2, nchunk - 1)
            for h in range(32):
                pend = []
                for j in range(nchunk):
                    pend.append(logits(h, j))
                    if j == jmid:
                        if h > 0:
                            stageC2(h - 1)
                        if h + 1 < 32:
                            stageA(h + 1)
                    if len(pend) > LA:
                        pv(pend.pop(0))
                while pend:
                    pv(pend.pop(0))
                stageC1(h)
            stageC2(31)

        def layer1(ti):
            norm_and_transpose([(xb1, b_xb1), (ost[0][0].bitcast(BF16)[:, 0:D], ost[0][1])], junk1, b_junk1, dg1, b_dg1, gb, b_gb)
            W = 16 + T
            def U(g):
                pooled, b_pooled = pooled2[g % 2]
                nst = g + 1
                win = 2 ** nst
                for cj in range(8):
                    ch = g * 8 + cj
                    wt, b_wt = load_w("wu", ch)
                    bk, b_bk = bank()
                    proj_fm(wt, b_wt, bk, b_bk)
                    ue_v, b_ue = ue[ch % 2]
                    if ti == 0:
                        k.op("pool", [], [b_ue], lambda e: e.memset(ue_v[:, 0:16], 0.0))
                    else:
                        k.op("pool", [b_halo], [b_ue], lambda e: e.tensor_copy(out=ue_v[:, 0:16], in_=halo[:, ch, :]))
                    k.op("dve", [b_bk, b_rbc], [b_ue], lambda e: e.tensor_tensor(out=ue_v[:, 16:W], in0=bk[:], in1=rbc[:], op=ALU.mult))
                    k.op("pool", [b_ue], [b_halo], lambda e: e.tensor_copy(out=halo[:, ch, :], in_=ue_v[:, T:W]))
                    cur, b_cur = ue_v, b_ue
                    for st in range(nst):
                        sh = 2 ** st
                        lo_ = 2 ** (st + 1)
                        dst, b_dst = (sB, b_sB) if st % 2 == 0 else (sC, b_sC)
                        k.op("pool", [b_cur], [b_dst],
                             lambda e: e.tensor_tensor(out=dst[:, lo_:W], in0=cur[:, lo_:W], in1=cur[:, lo_ - sh:W - sh], op=ALU.add))
                        cur, b_cur = dst, b_dst
                    k.op("dve", [b_cur, b_ue], [b_pooled],
                         lambda e: e.scalar_tensor_tensor(out=pooled[:, cj, :], in0=cur[:, 16:W], scalar=1.0 / win, in1=ue_v[:, 16:W], op0=ALU.mult, op1=ALU.subtract))
                    if ti == 0:
                        k.op("dve", [b_cur, b_icnt], [b_sml], lambda e: e.tensor_tensor(out=sml[:, 32:48], in0=cur[:, 16:32], in1=icnt[:, g * 16:(g + 1) * 16], op=ALU.mult))
                        k.op("dve", [b_sml, b_ue], [b_pooled], lambda e: e.tensor_tensor(out=pooled[:, cj, 0:16], in0=sml[:, 32:48], in1=ue_v[:, 16:32], op=ALU.subtract))

            def WZ(g):
                pooled, b_pooled = pooled2[g % 2]
                for qc in range(8):
                    ch = g * 8 + qc
                    i = ch % 2
                    k.dma("sp", wgb[i][:].rearrange("p c n -> p (c n)"), wdst["wg"][ch, :, :], [b_wd["wg"]], [b_wgb[i]], f"wg{i}")
                    bm, b_bm = bank()
                    for pc in range(8):
                        k.op("pe", [b_wgb[i], b_pooled], [b_bm], lambda e: e.matmul(bm[:], lhsT=wgb[i][:, pc, :], rhs=pooled[:, pc, :], start=(pc == 0), stop=(pc == 7)))
                    wt, b_wt = load_w("wz1", ch)
                    bz, b_bz = bank()
                    proj_fm(wt, b_wt, bz, b_bz)
                    silu_gate(bz, b_bz, zs1, b_zs1, ez1, b_ez1, gt1, b_gt1)
                    k.op("dve", [b_bm, b_bgrp, b_scb], [b_mx1],
                         lambda e: e.tensor_scalar(out=mx1[:, 0:T], in0=bm[:], scalar1=bgrp[:, ch:ch + 1], scalar2=scb[:, ch:ch + 1], op0=ALU.add, op1=ALU.mult))
                    k.op("dve", [b_mx1, b_gt1], [b_yT[ch % 16]], lambda e: e.tensor_tensor(out=yT[:, ch % 16, :], in0=mx1[:, 0:T], in1=gt1[:, 0:T], op=ALU.mult))
                if g % 2 == 1:
                    w_out_half("wo1", g // 2)


            U(0)
            for g in range(4):
                if g + 1 < 4:
                    U(g + 1)
                WZ(g)

        b_out = Buf("out")

        def final_norm_store(bi, ti, do_norm):
            if do_norm:
                k.dma("sp", gfin[:, 0:D], gfin_d[:, :], [], [b_gfin], "gfin")
            for tb in range(4):
                if do_norm:
                    k.op("act", [b_xs[tb]], [b_junk1, b_sml],
                         lambda e: e.activation(out=junk1[:, 0:D], in_=xs[:, tb, :], func=AF.Square, accum_out=sml[:, 20 + tb:21 + tb]))
                    k.op("act", [b_sml], [b_sml], lambda e: e.activation(out=sml[:, 24 + tb:25 + tb], in_=sml[:, 20 + tb:21 + tb], func=AF.Sqrt, scale=1.0 / D, bias=EPS))
                    k.op("dve", [b_sml], [b_sml], lambda e: e.reciprocal(out=sml[:, 28 + tb:29 + tb], in_=sml[:, 24 + tb:25 + tb]))
                    o_v, b_o = ost[tb % 2]
                    k.op("dve", [b_xs[tb], b_sml, b_gfin], [b_o],
                         lambda e: e.scalar_tensor_tensor(out=o_v[:, 0:D], in0=xs[:, tb, :], scalar=sml[:, 28 + tb:29 + tb], in1=gfin[:, 0:D], op0=ALU.mult, op1=ALU.mult))
                    r0 = ti * T + tb * 128
                    k.dma("sp", out_d[bi, r0:r0 + 128, :], o_v[:, 0:D], [b_o], [b_out], f"st{tb % 2}")
                else:
                    r0 = ti * T + tb * 128
                    k.dma("act", out_d[bi, r0:r0 + 128, :], xs[:, tb, :], [b_xs[tb]], [b_out], f"st{tb}")

        for bi in range(NB):
            for ti in range(NTILE):
                for tb in range(4):
                    r0 = ti * T + tb * 128
                    k.dma("act", xs[:, tb, :], x_d[bi, r0:r0 + 128, :], [], [b_xs[tb]], f"ld{tb}")
                if do_l0:
                    layer0(ti)
                    k.barrier(b_scr_all)
                if do_l1:
                    layer1(ti)
                final_norm_store(bi, ti, do_l1)
                if do_l1:
                    k.barrier(b_scr_all)
        fin_bufs = [b_out] + list(b_xs)
        if do_l1:
            fin_bufs += [b for (_, b) in ost]
        k.finish("sp", fin_bufs)
        k.finish("act", fin_bufs)
        print("instructions", k.nins, "waits", k.nwaits, {e: k.cnt[e] for e in k.cnt})
    return nc


def _t5_bucket_np(dist):
    d = np.maximum(dist, 0)
    df = np.maximum(d, 1).astype(np.float32)
    large = 16 + (np.log(df / 16) / np.log(128 / 16) * 16).astype(np.int32)
    large = np.minimum(large, 31)
    return np.where(d < 16, d, large)


def tile_cols(w, n):
    ncol = w.shape[1]
    nb = ncol // n
    return np.ascontiguousarray(w.reshape(16, 128, nb, n).transpose(2, 1, 0, 3)).reshape(nb, 128, 16 * n)


def tile_wout(w):
    return np.ascontiguousarray(w.reshape(2, 2, 8, 128, 4, 512).transpose(0, 4, 1, 3, 2, 5)).reshape(16, 128, 4096)


def host_layout(inp, do_l0=True, do_l1=True):
    f = lambda a: np.ascontiguousarray(np.asarray(a, dtype=np.float32))
    m = {}
    if do_l0:
        w_in = f(inp["w_in_a"][0])
        m["wq"] = tile_cols(w_in[:, 0:4096], 128)
        m["wz"] = tile_cols(w_in[:, 5456:9552], 128)
        m["wqi"] = tile_cols(w_in[:, 4352:5376], 128)
        wsm = np.concatenate([w_in[:, 4096:4352], w_in[:, 5376:5440], w_in[:, 5440:5456]], axis=1)
        m["wsm"] = tile_cols(wsm, 336)
        m["wuk"] = np.ascontiguousarray(f(inp["w_uk_a"][0]).transpose(1, 2, 0))
        m["wuv"] = np.ascontiguousarray(f(inp["w_uv_a"][0]).reshape(2, 128, 32, 128).transpose(2, 1, 0, 3)).reshape(32, 128, 256)
        m["wo"] = tile_wout(f(inp["w_out_a"][0]))
        rb = f(inp["rel_bias"])
        s = np.arange(128)[:, None]
        t = np.arange(128)[None, :]
        idx = np.concatenate([_t5_bucket_np(t - s), _t5_bucket_np(t - s + 128)], axis=1)
        m["bias"] = np.ascontiguousarray(rb[idx].transpose(2, 0, 1))
    if do_l1:
        w_in = f(inp["w_in_b"][0])
        m["wu"] = tile_cols(w_in[:, 0:4096], 128)
        m["wz1"] = tile_cols(w_in[:, 4096:8192], 128)
        wg = f(inp["w_grp_b"][0])
        m["wg"] = np.ascontiguousarray(wg.reshape(4, 8, 128, 8, 128).transpose(0, 3, 2, 1, 4)).reshape(32, 128, 1024)
        m["wo1"] = tile_wout(f(inp["w_out_b"][0]))
    colT = lambda v: np.ascontiguousarray(f(v).reshape(-1, 128).T)
    bc = lambda v: np.ascontiguousarray(np.broadcast_to(f(v).reshape(1, -1), (128, f(v).size)))
    m["ident"] = np.eye(128, dtype=np.float32)
    m["gaT"] = colT(inp["norm_a"][0])
    m["gbT"] = colT(inp["norm_b"][0])
    m["gkv"] = bc(inp["kv_norm_a"][0])
    m["gki"] = bc(inp["kidx_norm_a"][0])
    m["gfin"] = bc(inp["final_norm"])
    m["rb31"] = bc(f(inp["rel_bias"])[31])
    m["bgrp"] = colT(f(inp["b_grp_b"][0]).reshape(-1))
    m["scb"] = colT(inp["scale_b"][0])
    ss = np.arange(128)[None, :]
    tt = np.arange(128)[:, None]
    m["cneg"] = np.where(ss <= tt, 0.0, NEG).astype(np.float32)
    ic = np.zeros((128, 64), np.float32)
    for g in range(4):
        ic[:, g * 16:(g + 1) * 16] = 1.0 / np.minimum(np.arange(16) + 1, 2 ** (g + 1))
    m["icnt"] = ic
    m["cfs"] = np.ascontiguousarray(np.broadcast_to((0.5 ** (np.arange(32) + 1.0)).astype(np.float32)[None, :], (128, 32)))
    return m


_CACHE = {}


def kernel(**inputs):
    x = np.ascontiguousarray(np.asarray(inputs["x"], dtype=np.float32))
    B = x.shape[0]
    ncores = 8
    NB = B // ncores
    m = host_layout(inputs)
    if "nc" not in _CACHE:
        _CACHE["nc"] = build(NB)
    nc = _CACHE["nc"]
    in_maps = []
    for c in range(ncores):
        d = dict(m)
        d["x"] = x[c * NB:(c + 1) * NB]
        in_maps.append(d)
    res = run_bass_kernel_spmd(nc, in_maps, core_ids=list(range(ncores)))
    return np.concatenate([r["out"] for r in res.results], axis=0)
```
